# Optimizing a Trainium2 kernel written in Bass

```python
import math
import jax, jax.numpy as jnp
from jax import lax
import numpy as np

D_MODEL = 2048
BATCH = 8
SEQ = 2048
DEPTH = 2

N_MIXERS = 2
N_A = (DEPTH + 1) // 2
N_B = DEPTH // 2
BLK = 128
WIN = 128
KEY_SPAN = BLK + 2 * WIN
A_DH = 64
A_HQ = D_MODEL // A_DH
A_KV = 4
A_G = A_HQ // A_KV
A_QKV = A_HQ * A_DH + 2 * A_KV * A_DH
B_DH = 128
B_H = D_MODEL // (2 * B_DH)
B_QKV = 3 * B_H * 2 * B_DH
D_FF = int(math.ceil(8 * D_MODEL / 3 / 256) * 256)
PLE_DIM = 256
EPS = 1e-6
NEG = -1e30

kernel_name = "hybrid_swa_sink_diffattn_encoder"


def rms_norm(x, g):
    xf = x.astype(jnp.float32)
    y = xf * lax.rsqrt(jnp.mean(xf * xf, axis=-1, keepdims=True) + EPS)
    return (y * g.astype(jnp.float32)).astype(x.dtype)


def alibi_slopes(n):
    return jnp.exp2(-8.0 * jnp.arange(1, n + 1, dtype=jnp.float32) / n)


def windowed_gqa_sink(h, w_qkv, q_gain, k_gain, sink, w_o):
    B, S, _ = h.shape
    nb = S // BLK
    qkv = h @ w_qkv
    nq = A_HQ * A_DH
    nk = A_KV * A_DH
    q = rms_norm(qkv[..., :nq].reshape(B, S, A_KV, A_G, A_DH), q_gain)
    k = rms_norm(qkv[..., nq:nq + nk].reshape(B, S, A_KV, A_DH), k_gain)
    v = qkv[..., nq + nk:].reshape(B, S, A_KV, A_DH)
    qb = q.reshape(B, nb, BLK, A_KV, A_G, A_DH).transpose(1, 0, 3, 4, 2, 5)
    pad = ((0, 0), (0, 0), (WIN, WIN), (0, 0))
    kp = jnp.pad(k.transpose(0, 2, 1, 3), pad)
    vp = jnp.pad(v.transpose(0, 2, 1, 3), pad)
    slopes = alibi_slopes(A_HQ).reshape(A_KV, A_G)
    sink_f = sink.astype(jnp.float32).reshape(A_KV, A_G)
    scale = A_DH ** -0.5

    def block(args):
        qblk, n = args
        start = n * BLK
        kb = lax.dynamic_slice_in_dim(kp, start, KEY_SPAN, axis=2)
        vb = lax.dynamic_slice_in_dim(vp, start, KEY_SPAN, axis=2)
        sc = jnp.einsum('bkgqd,bksd->bkgqs', qblk, kb).astype(jnp.float32) * scale
        t = start + jnp.arange(BLK)
        s = start - WIN + jnp.arange(KEY_SPAN)
        dist = jnp.abs(t[:, None] - s[None, :])
        valid = (dist <= WIN) & (s[None, :] >= 0) & (s[None, :] < S)
        sc = sc - slopes[:, :, None, None] * dist.astype(jnp.float32)
        sc = jnp.where(valid, sc, NEG)
        sink_col = jnp.broadcast_to(sink_f[None, :, :, None, None], sc.shape[:-1] + (1,))
        probs = jax.nn.softmax(jnp.concatenate([sc, sink_col], axis=-1), axis=-1)[..., :-1]
        return jnp.einsum('bkgqs,bksd->bkgqd', probs.astype(vb.dtype), vb)

    o = lax.map(block, (qb, jnp.arange(nb)))
    o = o.transpose(1, 0, 4, 2, 3, 5).reshape(B, S, A_HQ * A_DH)
    return o @ w_o


def diff_attention(h, w_qkv, q_gain, k_gain, lam_vecs, subln, w_o, lambda_init):
    B, S, _ = h.shape
    nb = S // BLK
    qkv = h @ w_qkv
    w = B_H * 2 * B_DH
    q = rms_norm(qkv[..., :w].reshape(B, S, B_H, 2, B_DH), q_gain)
    k = rms_norm(qkv[..., w:2 * w].reshape(B, S, B_H, 2, B_DH), k_gain)
    v = qkv[..., 2 * w:].reshape(B, S, B_H, 2 * B_DH)
    qb = q.reshape(B, nb, BLK, B_H, 2, B_DH).transpose(1, 0, 3, 4, 2, 5)
    kt = k.transpose(0, 2, 3, 1, 4)
    vt = v.transpose(0, 2, 1, 3)
    lv = lam_vecs.astype(jnp.float32)
    lam = jnp.exp(jnp.sum(lv[0] * lv[1])) - jnp.exp(jnp.sum(lv[2] * lv[3])) + lambda_init
    slopes = alibi_slopes(B_H)
    s_pos = jnp.arange(S)
    scale = B_DH ** -0.5

    def block(args):
        qblk, n = args
        sc = jnp.einsum('bhcqd,bhcsd->bhcqs', qblk, kt).astype(jnp.float32) * scale
        t = n * BLK + jnp.arange(BLK)
        dist = jnp.abs(t[:, None] - s_pos[None, :]).astype(jnp.float32)
        sc = sc - (slopes[:, None, None] * dist)[None, :, None]
        probs = jax.nn.softmax(sc, axis=-1)
        wts = probs[:, :, 0] - lam * probs[:, :, 1]
        return jnp.einsum('bhqs,bhse->bhqe', wts.astype(vt.dtype), vt)

    o = lax.map(block, (qb, jnp.arange(nb)))
    o = rms_norm(o, subln) * (1.0 - lambda_init)
    o = o.transpose(1, 0, 3, 2, 4).reshape(B, S, B_H * 2 * B_DH)
    return o @ w_o


def swiglu(h, w_in, w_out):
    gu = h @ w_in
    return (jax.nn.silu(gu[..., :D_FF]) * gu[..., D_FF:]) @ w_out


def setup_inputs(seed: int = 0) -> dict:
    key = jax.random.key(seed)
    ks = jax.random.split(key, 24)
    f32 = jnp.float32

    def nrm(k, shape, scale):
        return jax.random.normal(k, shape, f32) * scale

    def gain(k, shape):
        return 1.0 + 0.02 * jax.random.normal(k, shape, f32)

    return {
        "x": nrm(ks[0], (BATCH, SEQ, D_MODEL), 1.0),
        "p": nrm(ks[1], (DEPTH, BATCH, SEQ, PLE_DIM), 1.0),
        "attn_norm": gain(ks[2], (DEPTH, D_MODEL)),
        "ffn_norm": gain(ks[3], (DEPTH, D_MODEL)),
        "a_w_qkv": nrm(ks[4], (N_A, D_MODEL, A_QKV), D_MODEL ** -0.5),
        "a_q_norm": gain(ks[5], (N_A, A_DH)),
        "a_k_norm": gain(ks[6], (N_A, A_DH)),
        "a_sink": nrm(ks[7], (N_A, A_HQ), 0.5),
        "a_w_o": nrm(ks[8], (N_A, A_HQ * A_DH, D_MODEL), (A_HQ * A_DH) ** -0.5),
        "b_w_qkv": nrm(ks[9], (N_B, D_MODEL, B_QKV), D_MODEL ** -0.5),
        "b_q_norm": gain(ks[10], (N_B, B_DH)),
        "b_k_norm": gain(ks[11], (N_B, B_DH)),
        "b_lambda": nrm(ks[12], (N_B, 4, B_DH), 0.1),
        "b_subln": gain(ks[13], (N_B, 2 * B_DH)),
        "b_w_o": nrm(ks[14], (N_B, B_H * 2 * B_DH, D_MODEL), (B_H * 2 * B_DH) ** -0.5),
        "w_ffn_in": nrm(ks[15], (DEPTH, D_MODEL, 2 * D_FF), D_MODEL ** -0.5),
        "w_ffn_out": nrm(ks[16], (DEPTH, D_FF, D_MODEL), D_FF ** -0.5),
        "ple_w_proj": nrm(ks[17], (DEPTH, PLE_DIM, D_MODEL), PLE_DIM ** -0.5),
        "ple_post_norm": gain(ks[18], (DEPTH, D_MODEL)),
        "ple_gate_norm": gain(ks[19], (DEPTH, D_MODEL)),
        "ple_w_gate": nrm(ks[20], (DEPTH, D_MODEL, D_MODEL), D_MODEL ** -0.5),
    }


def reference(x, p, attn_norm, ffn_norm, a_w_qkv, a_q_norm, a_k_norm, a_sink, a_w_o,
              b_w_qkv, b_q_norm, b_k_norm, b_lambda, b_subln, b_w_o,
              w_ffn_in, w_ffn_out, ple_w_proj, ple_post_norm, ple_gate_norm, ple_w_gate):
    h = x
    for i in range(DEPTH):
        hn = rms_norm(h, attn_norm[i])
        j = i // N_MIXERS
        if i % N_MIXERS == 0:
            mix = windowed_gqa_sink(hn, a_w_qkv[j], a_q_norm[j], a_k_norm[j], a_sink[j], a_w_o[j])
        else:
            lambda_init = 0.8 - 0.6 * math.exp(-0.3 * i)
            mix = diff_attention(hn, b_w_qkv[j], b_q_norm[j], b_k_norm[j], b_lambda[j],
                                 b_subln[j], b_w_o[j], lambda_init)
        h = h + mix
        h = h + swiglu(rms_norm(h, ffn_norm[i]), w_ffn_in[i], w_ffn_out[i])
        gate = jax.nn.sigmoid(rms_norm(h, ple_gate_norm[i]) @ ple_w_gate[i])
        h = h + rms_norm(p[i] @ ple_w_proj[i], ple_post_norm[i]) * gate
    return h
```

```python
import math
import os
from contextlib import ExitStack

import numpy as np
import concourse.bass as bass
import concourse.mybir as mybir
from concourse.bass_utils import run_bass_kernel_spmd

F32 = mybir.dt.float32
BF16 = mybir.dt.bfloat16
U16 = mybir.dt.uint16
AF = mybir.ActivationFunctionType
ALU = mybir.AluOpType
AX = mybir.AxisListType

T = 2048
D = 2048
NT = 16
DC = 16
DFF = 5632
FC = 44
EPS = 1e-6
GOFF = 1920
GW = 3968
WEL = 12288
REL = 61440


class Q:
    def __init__(self, K, eng, name):
        self.K = K
        self.eng = eng
        self.sem = K.sem("q_" + name)
        self.cnt = 0
        self.seen = {}

    def wait(self, *evs):
        for ev in evs:
            if ev is None:
                continue
            if isinstance(ev, list):
                self.wait(*ev)
                continue
            sem, val = ev
            k = sem.num
            if self.seen.get(k, 0) >= val:
                continue
            self.eng.wait_ge(sem, val)
            self.seen[k] = val

    def mark(self, ins):
        self.cnt += 1
        ins.then_inc(self.sem, 1)
        return (self.sem, self.cnt)


class Slot:
    def __init__(self, K, name):
        self.sem = K.sem("d_" + name)
        self.cnt = 0

    def dma(self, q, out, in_, **kw):
        self.cnt += 16
        q.eng.dma_start(out=out, in_=in_, **kw).then_inc(self.sem, 16)
        return (self.sem, self.cnt)


class Kern:
    def __init__(self):
        self.nc = bass.Bass("TRN2", target_bir_lowering=False)
        self.es = ExitStack()
        nc = self.nc
        self.pe = Q(self, nc.tensor, "pe")
        self.act = Q(self, nc.scalar, "act")
        self.dve = Q(self, nc.vector, "dve")
        self.pool = Q(self, nc.gpsimd, "pool")
        self.sp = Q(self, nc.sync, "sp")
        self.ps = self.es.enter_context(nc.psum_tensor("ps", [128, 8, 512], F32))
        self.bank_free = [None] * 8
        self.bank_i = 0

    def sem(self, name):
        return self.es.enter_context(self.nc.semaphore(name))

    def sb(self, name, shape, dt):
        return self.es.enter_context(self.nc.sbuf_tensor(name, shape, dt))

    def next_bank(self, lo=0, hi=8):
        n = hi - lo
        b = lo + (self.bank_i % n)
        self.bank_i += 1
        return b

    def bank(self, b):
        return self.ps[:, b, :]

    def bank_bf(self, b):
        return self.ps[:, b, :].bitcast(BF16)


class WStream:
    def __init__(self, K):
        self.K = K
        self.buf = [K.sb(f"wbuf{i}", [128, WEL], BF16) for i in range(2)]
        self.slot = [Slot(K, f"w{i}") for i in range(2)]
        self.free = [None, None]
        self.plan = []
        self.issued = 0
        self.used = 0
        self.evs = []

    def add(self, tag, parts, kc, nb):
        assert kc * nb <= WEL
        self.plan.append((tag, parts, kc, nb))

    def view(self, i):
        _, _, kc, nb = self.plan[i]
        return self.buf[i % 2][:, 0:kc * nb].rearrange("p (k n) -> p k n", n=nb)

    def _issue(self, i):
        s = i % 2
        tag, parts, kc, nb = self.plan[i]
        self.K.pool.wait(self.free[s])
        v = self.view(i)
        ev = None
        for (lo, hi, src) in parts:
            ev = self.slot[s].dma(self.K.pool, v[:, :, lo:hi], src)
        self.evs.append(ev)

    def acquire(self, tag):
        i = self.used
        assert self.plan[i][0] == tag, (self.plan[i][0], tag)
        while self.issued < len(self.plan) and self.issued <= i + 1:
            self._issue(self.issued)
            self.issued += 1
        return self.view(i), self.evs[i]

    def release(self, ev):
        self.free[self.used % 2] = ev
        self.used += 1


def wsrc(w, c0, c1):
    return w.rearrange("(kc p) n -> p kc n", p=128)[:, :, c0:c1]


def alibi_slopes(n):
    return [2.0 ** (-8.0 * (i + 1) / n) for i in range(n)]


def build(stop=None):
    K = Kern()
    nc = K.nc
    pe, act, dve, pool, sp = K.pe, K.act, K.dve, K.pool, K.sp
    ps = K.ps

    def din(name, shape, dt=F32):
        return nc.dram_tensor(name, list(shape), dt, kind="ExternalInput").ap()

    x = din("x", [T, D])
    pT = din("pT", [2, 256, T])
    a_w_qkv = din("a_w_qkv", [D, 2560])
    a_w_o = din("a_w_o", [D, D])
    b_w_qkv = din("b_w_qkv", [D, 6144])
    b_w_o = din("b_w_o", [D, D])
    w_ffn_in = din("w_ffn_in", [2, D, 2 * DFF])
    w_ffn_out = din("w_ffn_out", [2, DFF, D])
    ple_w_proj = din("ple_w_proj", [2, 256, D])
    ple_w_gate = din("ple_w_gate", [2, D, D])
    gcols_d = din("gcols", [128, 6 * 16])
    postg_d = din("postg", [2, D])
    vecs_d = din("vecs", [1184])
    ident_d = din("ident", [128, 128])
    distA_d = din("distA", [128, 384], U16)
    distB_d = din("distB", [128, GW], U16)
    out = nc.dram_tensor("out", [T, D], F32, kind="ExternalOutput").ap()
    hres = nc.dram_tensor("hres", [T, D], F32).ap()
    obuf = nc.dram_tensor("obuf", [T, D], BF16).ap()

    R = K.sb("R", [128, REL], BF16)
    rtb = K.sb("rtb", [128, 2, 2048], F32)
    junk = K.sb("junk", [128, 1024], BF16)
    pcs = K.sb("pcs", [128, 4, 512], F32)
    scs = K.sb("scs", [128, 2, 512], F32)
    qbs = K.sb("qbs", [128, 2, 512], BF16)
    identF = K.sb("identF", [128, 128], F32)
    identB = K.sb("identB", [128, 128], BF16)
    gcols = K.sb("gcols_sb", [128, 96], F32)
    vecs = K.sb("vecs_sb", [128, 1184], F32)
    distA = K.sb("distA_sb", [128, 384], U16)
    small = K.sb("small", [128, 256], F32)
    epsc = K.sb("epsc", [128, 8], F32)
    W = WStream(K)

    def Rv(lo, n, inner):
        return R[:, lo:lo + n].rearrange("p (c t) -> p c t", t=inner)

    hT = Rv(0, 32768, 2048)
    QTg = Rv(32768, 8192, 2048)
    KTd = Rv(40960, 8192, 2048)
    VA = R[:, 49152:49152 + 4160].rearrange("p (t h d) -> p t h d", h=4, d=65)
    QTh = Rv(32768, 4096, 2048)
    KTh = Rv(36864, 4096, 2048)
    VB = R[:, 40960:40960 + 4112].rearrange("p (t d) -> p t d", d=257)
    GT = R[:, 53312:53312 + GW]
    GTu = GT.bitcast(U16)
    EP = R[:, 57280:57280 + 2048].rearrange("p (s n) -> p s n", n=512)
    actT = Rv(0, 45056, 1024)
    hTh = Rv(45056, 16384, 1024)
    pTs = Rv(32768, 4096, 2048)
    wpj = Rv(36864, 4096, 2048)
    wpj2 = Rv(40960, 4096, 2048)
    og = rtb[:, :, :].rearrange("p a b -> p (a b)").bitcast(BF16)[:, 0:8192].rearrange("p (t c) -> p t c", c=512)

    SS = 0
    RS = 16
    S8 = 32
    R8 = 48
    ESK = 64
    DEN = 96
    LAM = 104
    RP = 112
    SP4 = 128
    SSB = 192
    V_AQ, V_AK, V_SINK, V_BQ, V_BK, V_LAM, V_SUB = 0, 64, 128, 160, 288, 416, 928

    for l in range(2):
        if l == 0:
            W.add("a_kv", [(0, 256, wsrc(a_w_qkv, 2048, 2304)), (256, 512, wsrc(a_w_qkv, 2304, 2560))], 16, 512)
            for g in range(4):
                W.add(f"a_q{g}", [(0, 512, wsrc(a_w_qkv, g * 512, (g + 1) * 512))], 16, 512)
            wo = a_w_o
        else:
            for h in range(8):
                W.add(f"b_h{h}", [(0, 256, wsrc(b_w_qkv, h * 256, (h + 1) * 256)),
                                  (256, 512, wsrc(b_w_qkv, 2048 + h * 256, 2048 + (h + 1) * 256)),
                                  (512, 768, wsrc(b_w_qkv, 4096 + h * 256, 4096 + (h + 1) * 256))], 16, 768)
            wo = b_w_o
        for b in range(4):
            W.add(f"wo{l}_{b}", [(0, 512, wsrc(wo, b * 512, (b + 1) * 512))], 16, 512)
        for hf in range(2):
            for blk in range(22):
                W.add(f"fi{l}_{hf}_{blk}", [(0, 256, wsrc(w_ffn_in[l], blk * 256, (blk + 1) * 256)),
                                            (256, 512, wsrc(w_ffn_in[l], DFF + blk * 256, DFF + (blk + 1) * 256))], 16, 512)
            for b in range(8):
                W.add(f"fo{l}_{hf}_{b}", [(0, 256, wsrc(w_ffn_out[l], b * 256, (b + 1) * 256))], FC, 256)
        for b in range(4):
            W.add(f"wg{l}_{b}", [(0, 512, wsrc(ple_w_gate[l], b * 512, (b + 1) * 512))], 16, 512)

    cs = Slot(K, "const")
    cev = cs.dma(sp, identF[:], ident_d)
    cev = cs.dma(sp, gcols[:], gcols_d)
    cev = cs.dma(sp, vecs[:], vecs_d.partition_broadcast(128))
    cev = cs.dma(sp, distA[:], distA_d)
    cs2 = Slot(K, "const2")
    cev2 = cs2.dma(pool, identB[:], ident_d)
    dve.wait(cev)
    e = dve.mark(nc.vector.memset(small[:], 0.0))
    dve.wait(e)
    e = dve.mark(nc.vector.memset(epsc[:], EPS))
    act.wait(cev, e)
    esk_ev = act.mark(nc.scalar.activation(out=small[:, ESK:ESK + 32], in_=vecs[:, V_SINK:V_SINK + 32], func=AF.Exp))

    res_ev = [[None] * 8 for _ in range(NT)]
    st = {"rt_i": 0, "pc_i": 0, "sc_i": 0, "ev_i": 0}
    rt_slot = [Slot(K, "rt0"), Slot(K, "rt1")]
    rt_free = [None, None]
    junk_free = [None]
    pc_slot_in = [Slot(K, f"pci{i}") for i in range(4)]
    pc_slot_out = [Slot(K, f"pco{i}") for i in range(4)]
    pc_free = [None] * 4
    sc_free = [None, None]
    qb_free = [None, None]

    def res_src(l):
        return x if l == 0 else hres

    def evac_engine():
        st["ev_i"] += 1
        return act if st["ev_i"] % 2 == 0 else dve

    def rstd_op(dst, src_ap, n_inv, waits):
        act.wait(waits)
        e = act.mark(nc.scalar.activation(out=dst, in_=src_ap, func=AF.Sqrt, scale=n_inv, bias=epsc[:, 0:1]))
        dve.wait(e)
        return dve.mark(nc.vector.reciprocal(out=dst, in_=dst))

    def copy_on(q, out_ap, in_ap):
        if q is act:
            return nc.scalar.copy(out=out_ap, in_=in_ap)
        return nc.vector.tensor_copy(out=out_ap, in_=in_ap)

    class Pro:
        def __init__(self, src, tiles, gci, dstT, tok0, dst_free_ev, bf16_src=False, src_wait=None, emit_all=True, dist=3):
            self.src, self.tiles, self.gci, self.dstT, self.tok0 = src, list(tiles), gci, dstT, tok0
            self.dst_free_ev, self.bf16_src, self.src_wait = dst_free_ev, bf16_src, src_wait
            self.ready = {}
            self.todo = list(tiles)
            self.inflight = []
            self.ld = {}
            self.k = 0
            self.dist = dist
            if emit_all:
                self.emit_rest()

        def _force_one(self):
            if self.inflight:
                tt, _ = self.inflight.pop(0)
                self.emit_pe(tt)
            elif self.todo:
                tt = self.todo.pop(0)
                self.emit_load(tt)
                self.emit_pe(tt)

        def emit_rest(self):
            while self.todo or self.inflight:
                self._force_one()

        def emit_next(self, n=1):
            for _ in range(n):
                self._force_one()

        def need(self, tt):
            while tt not in self.ready:
                self._force_one()

        def step(self, max_tile=None):
            self.k += 1
            while self.inflight and self.k - self.inflight[0][1] >= self.dist:
                tt, _ = self.inflight.pop(0)
                self.emit_pe(tt)
            while self.todo and len(self.inflight) < 2 and (max_tile is None or self.todo[0] <= max_tile):
                tt = self.todo.pop(0)
                self.emit_load(tt)
                self.inflight.append((tt, self.k))

        def rdy(self, tts):
            return [self.ready[t] for t in tts]

        def emit_load(self, tt):
            src = self.src
            s = st["rt_i"] % 2
            st["rt_i"] += 1
            sp.wait(rt_free[s], res_ev[tt] if self.src_wait is None else self.src_wait)
            if self.bf16_src:
                rt_bf = rtb[:, s, :].bitcast(BF16)[:, 0:2048]
                ev_ld = rt_slot[s].dma(sp, rt_bf, src[tt * 128:(tt + 1) * 128, :])
                self.ld[tt] = (s, ev_ld, rt_bf)
            else:
                rt = rtb[:, s, :]
                ev_ld = rt_slot[s].dma(sp, rt, src[tt * 128:(tt + 1) * 128, :])
                col = tt
                act.wait(ev_ld, junk_free[0])
                e1 = act.mark(nc.scalar.activation(out=junk[:], in_=rt[:, 0:1024], func=AF.Square,
                                                   accum_out=small[:, SS + col:SS + col + 1]))
                act.wait(e1)
                e1 = act.mark(nc.scalar.activation(out=junk[:], in_=rt[:, 1024:2048], func=AF.Square,
                                                   accum_out=small[:, SSB + col:SSB + col + 1]))
                junk_free[0] = e1
                dve.wait(e1)
                e2 = dve.mark(nc.vector.tensor_tensor(out=small[:, SS + col:SS + col + 1], in0=small[:, SS + col:SS + col + 1],
                                                      in1=small[:, SSB + col:SSB + col + 1], op=ALU.add))
                e3 = rstd_op(small[:, RS + col:RS + col + 1], small[:, SS + col:SS + col + 1], 1.0 / D, e2)
                dve.wait(e3)
                e4 = dve.mark(nc.vector.tensor_scalar(out=rt, in0=rt, scalar1=small[:, RS + col:RS + col + 1],
                                                      scalar2=None, op0=ALU.mult))
                self.ld[tt] = (s, e4, rt)

        def emit_pe(self, tt):
            gci, dstT, tok0, dst_free_ev = self.gci, self.dstT, self.tok0, self.dst_free_ev
            s, ready, rt = self.ld.pop(tt)
            t0 = tt * 128 - tok0
            evs = []
            if self.bf16_src:
                rt_bf = rt
                for cg in range(2):
                    bk = K.next_bank()
                    pe.wait(K.bank_free[bk], ready, cev2)
                    pb = K.bank_bf(bk)
                    for j in range(8):
                        c = cg * 8 + j
                        mm = nc.tensor.transpose(out=pb[:, j * 128:(j + 1) * 128], in_=rt_bf[:, c * 128:(c + 1) * 128],
                                                 identity=identB[:])
                    pev = pe.mark(mm)
                    q = evac_engine()
                    q.wait(pev, dst_free_ev)
                    e5 = q.mark(copy_on(q, dstT[:, cg * 8:(cg + 1) * 8, t0:t0 + 128],
                                        pb.rearrange("p (c t) -> p c t", t=128)))
                    K.bank_free[bk] = e5
                    evs.append(e5)
                rt_free[s] = pev
            else:
                for cg in range(4):
                    bk = K.next_bank()
                    pe.wait(K.bank_free[bk], ready, cev)
                    for j in range(4):
                        c = cg * 4 + j
                        mm = nc.tensor.transpose(out=ps[:, bk, j * 128:(j + 1) * 128], in_=rt[:, c * 128:(c + 1) * 128],
                                                 identity=identF[:])
                    pev = pe.mark(mm)
                    dve.wait(pev, dst_free_ev)
                    gb = gcols[:, gci * 16 + cg * 4:gci * 16 + cg * 4 + 4].unsqueeze(2).to_broadcast([128, 4, 128])
                    e5 = dve.mark(nc.vector.tensor_tensor(out=dstT[:, cg * 4:(cg + 1) * 4, t0:t0 + 128],
                                                          in0=ps[:, bk, :].rearrange("p (c t) -> p c t", t=128),
                                                          in1=gb, op=ALU.mult))
                    K.bank_free[bk] = e5
                    evs.append(e5)
                rt_free[s] = pev
            self.ready[tt] = evs

    class ResPieces:
        def __init__(self, l_src, dst, ncol, order, ahead=2):
            self.l_src, self.dst, self.ncol, self.order, self.ahead = l_src, dst, ncol, order, ahead
            self.loaded = []

        def _load(self, i):
            tt, c0 = self.order[i]
            s = st["pc_i"] % 4
            st["pc_i"] += 1
            pk = [c0 // 256 + k for k in range(self.ncol // 256)]
            sp.wait(pc_free[s], [res_ev[tt][k] for k in pk])
            ev_in = pc_slot_in[s].dma(sp, pcs[:, s, 0:self.ncol], self.l_src[tt * 128:(tt + 1) * 128, c0:c0 + self.ncol])
            self.loaded.append((s, ev_in))

        def finish(self, i, psum_ap, wait_evs):
            while len(self.loaded) < min(len(self.order), i + 1 + self.ahead):
                self._load(len(self.loaded))
            tt, c0 = self.order[i]
            s, ev_in = self.loaded[i]
            pc = pcs[:, s, 0:self.ncol]
            pk = [c0 // 256 + k for k in range(self.ncol // 256)]
            dve.wait(ev_in, wait_evs)
            e = dve.mark(nc.vector.tensor_tensor(out=pc, in0=psum_ap, in1=pc, op=ALU.add))
            sp.wait(e)
            ev_out = pc_slot_out[s].dma(sp, self.dst[tt * 128:(tt + 1) * 128, c0:c0 + self.ncol], pc)
            pc_free[s] = ev_out
            for k in pk:
                res_ev[tt][k] = ev_out
            return e

    def stop_here(tag):
        if stop != tag:
            return False
        allw = [ev for row in res_ev for ev in row]
        sp.wait(allw)
        ds = Slot(K, "dbg")
        e = ds.dma(sp, out, hres)
        sp.wait(e)
        return True

    LA = 3
    NEP = 8
    EP8 = R[:, 57280:57280 + 4096].rearrange("p (s n) -> p s n", n=512)
    ep_free = [None] * NEP
    epi = {"i": 0}
    proF = {}

    for l in range(2):
        src = res_src(l)
        if "proA_next" in st:
            proA = st.pop("proA_next")
            proA.dst_free_ev = st.get("hT_free")
            proA.step()
        else:
            proA = Pro(src, list(range(NT)), 0 + l, hT, 0, st.get("hT_free"), emit_all=False, dist=1)
            proA.step()
            proA.step()
        if l == 0:
            slopes = alibi_slopes(32)
            dve.wait(cev)
            ones_ev = dve.mark(nc.vector.memset(VA[:, :, :, 64:65], 1.0))
            gqk_ev = dve.mark(nc.vector.tensor_tensor(out=vecs[:, V_AK:V_AK + 64], in0=vecs[:, V_AK:V_AK + 64],
                                                      in1=vecs[:, V_AQ:V_AQ + 64], op=ALU.mult))
            wv, wev = W.acquire("a_kv")
            last_pe = None
            kv_ready = []
            pend = None
            for tt in range(NT):
                proA.need(tt)
                proA.step()
                bk = K.next_bank()
                pe.wait(K.bank_free[bk], wev, proA.ready[tt])
                for kc in range(DC):
                    mm = nc.tensor.matmul(ps[:, bk, :], lhsT=hT[:, kc, tt * 128:(tt + 1) * 128], rhs=wv[:, kc, :],
                                          start=(kc == 0), stop=(kc == DC - 1))
                pev = pe.mark(mm)
                last_pe = pev
                if pend is not None:
                    pend()
                s = st["sc_i"] % 2
                st["sc_i"] += 1
                sc = scs[:, s, 0:256]
                act.wait(pev, sc_free[s])
                e1 = act.mark(nc.scalar.activation(out=sc, in_=ps[:, bk, 0:256], func=AF.Square))
                act.wait(ones_ev)
                e1v = act.mark(nc.scalar.copy(out=VA[:, tt, :, 0:64], in_=ps[:, bk, 256:512].rearrange("p (h d) -> p h d", d=64)))
                kv_ready.append(e1v)
                dve.wait(e1)
                e2 = dve.mark(nc.vector.tensor_reduce(out=small[:, S8:S8 + 4], in_=sc.rearrange("p (h d) -> p h d", d=64),
                                                      axis=AX.X, op=ALU.add))
                e4 = rstd_op(small[:, R8:R8 + 4], small[:, S8:S8 + 4], 1.0 / 64, e2)
                dve.wait(e4)
                e5 = dve.mark(nc.vector.tensor_tensor(out=sc.rearrange("p (h d) -> p h d", d=64),
                                                      in0=ps[:, bk, 0:256].rearrange("p (h d) -> p h d", d=64),
                                                      in1=small[:, R8:R8 + 4].unsqueeze(2).to_broadcast([128, 4, 64]),
                                                      op=ALU.mult))
                K.bank_free[bk] = [e5, e1v]
                dve.wait(e5, qb_free[s], gqk_ev)
                qb = qbs[:, s, :]
                e6 = dve.mark(nc.vector.tensor_tensor(
                    out=qb.rearrange("p (h r d) -> p h r d", r=2, d=64),
                    in0=sc.rearrange("p (h d) -> p h d", d=64).unsqueeze(2).to_broadcast([128, 4, 2, 64]),
                    in1=vecs[:, V_AK:V_AK + 64].unsqueeze(1).unsqueeze(1).to_broadcast([128, 4, 2, 64]),
                    op=ALU.mult))
                sc_free[s] = e6

                def fin(tt=tt, s=s, qb=qb, e6=e6):
                    bk2 = K.next_bank()
                    pe.wait(K.bank_free[bk2], e6, cev2)
                    pb = K.bank_bf(bk2)
                    for j in range(4):
                        mm2 = nc.tensor.transpose(out=pb[:, j * 128:(j + 1) * 128], in_=qb[:, j * 128:(j + 1) * 128], identity=identB[:])
                    pev2 = pe.mark(mm2)
                    qb_free[s] = pev2
                    q = evac_engine()
                    q.wait(pev2)
                    e7 = q.mark(copy_on(q, KTd[:, :, tt * 128:(tt + 1) * 128], pb[:, 0:512].rearrange("p (c t) -> p c t", t=128)))
                    K.bank_free[bk2] = e7
                    kv_ready.append(e7)
                pend = fin
            pend()
            proA.emit_rest()
            W.release(last_pe)
            og_free = None
            g_free = None
            pairc = {"s": 0, "o": 0, "g": 0}
            g_free2 = [None, None]
            for g in range(4):
                wv, wev = W.acquire(f"a_q{g}")
                last_pe = None
                q_ready = []
                pend = None
                for tt in range(NT):
                    bk = K.next_bank()
                    pe.wait(K.bank_free[bk], wev, proA.ready[tt])
                    for kc in range(DC):
                        mm = nc.tensor.matmul(ps[:, bk, :], lhsT=hT[:, kc, tt * 128:(tt + 1) * 128], rhs=wv[:, kc, :],
                                              start=(kc == 0), stop=(kc == DC - 1))
                    pev = pe.mark(mm)
                    last_pe = pev
                    if pend is not None:
                        pend()
                    s = st["sc_i"] % 2
                    st["sc_i"] += 1
                    sc = scs[:, s, :]
                    act.wait(pev, sc_free[s])
                    e1 = act.mark(nc.scalar.activation(out=sc, in_=ps[:, bk, :], func=AF.Square))
                    dve.wait(e1)
                    e2 = dve.mark(nc.vector.tensor_reduce(out=small[:, S8:S8 + 8], in_=sc.rearrange("p (h d) -> p h d", d=64),
                                                          axis=AX.X, op=ALU.add))
                    e4 = rstd_op(small[:, R8:R8 + 8], small[:, S8:S8 + 8], 1.0 / 64, e2)
                    dve.wait(e4, qb_free[s])
                    qb = qbs[:, s, :]
                    e6 = dve.mark(nc.vector.tensor_tensor(out=qb.rearrange("p (h d) -> p h d", d=64),
                                                          in0=ps[:, bk, :].rearrange("p (h d) -> p h d", d=64),
                                                          in1=small[:, R8:R8 + 8].unsqueeze(2).to_broadcast([128, 8, 64]),
                                                          op=ALU.mult))
                    K.bank_free[bk] = e6
                    sc_free[s] = e2

                    def fin(tt=tt, s=s, qb=qb, e6=e6):
                        bk2 = K.next_bank()
                        pe.wait(K.bank_free[bk2], e6, cev2)
                        pb = K.bank_bf(bk2)
                        for j in range(4):
                            mm2 = nc.tensor.transpose(out=pb[:, j * 128:(j + 1) * 128], in_=qb[:, j * 128:(j + 1) * 128], identity=identB[:])
                        pev2 = pe.mark(mm2)
                        qb_free[s] = pev2
                        q = evac_engine()
                        q.wait(pev2, st.get("qt_free"))
                        e7 = q.mark(copy_on(q, QTg[:, :, tt * 128:(tt + 1) * 128], pb[:, 0:512].rearrange("p (c t) -> p c t", t=128)))
                        K.bank_free[bk2] = e7
                        q_ready.append(e7)
                    pend = fin
                pend()
                W.release(last_pe)
                for hp in range(4):
                    hA = g * 8 + 2 * hp
                    c = hp
                    gb_ = pairc["g"] % 2
                    pairc["g"] += 1
                    go = gb_ * 768
                    act.wait(cev, g_free2[gb_])
                    g_ev = act.mark(nc.scalar.activation(out=GT[:, go:go + 384], in_=distA[:], func=AF.Exp, scale=-slopes[hA]))
                    g_ev = act.mark(nc.scalar.activation(out=GT[:, go + 384:go + 768], in_=distA[:], func=AF.Exp, scale=-slopes[hA + 1]))
                    GT2 = GT[:, go:go + 768].rearrange("p (r n) -> p r n", n=384)
                    pq = []
                    o_last = None
                    LAP = 3
                    for i in range(NT + LAP):
                        if i < NT:
                            n = i
                            j0, j1 = max(n - 1, 0), min(n + 1, NT - 1)
                            a, b_ = (j0 - n + 1) * 128, (j1 - n + 2) * 128
                            sb = 2 * (pairc["s"] % 2)
                            pairc["s"] += 1
                            pe.wait(K.bank_free[sb], K.bank_free[sb + 1], q_ready, kv_ready)
                            for j in range(j0, j1 + 1):
                                cc = (j - n + 1) * 128
                                for r in range(2):
                                    mm = nc.tensor.matmul(ps[:, sb + r, cc:cc + 128],
                                                          lhsT=KTd[r * 64:(r + 1) * 64, g, j * 128:(j + 1) * 128],
                                                          rhs=QTg[r * 64:(r + 1) * 64, c, n * 128:(n + 1) * 128], start=True, stop=True)
                            pev = pe.mark(mm)
                            es = 2 * (epi["i"] % 4)
                            epi["i"] += 1
                            act.wait(pev, ep_free[es], ep_free[es + 1])
                            e1 = act.mark(nc.scalar.activation(out=EP8[:, es:es + 2, a:b_], in_=ps[:, sb:sb + 2, a:b_], func=AF.Exp, scale=0.125))
                            K.bank_free[sb] = e1
                            K.bank_free[sb + 1] = e1
                            dve.wait(e1, g_ev)
                            e2 = dve.mark(nc.vector.tensor_tensor(out=EP8[:, es:es + 2, a:b_], in0=EP8[:, es:es + 2, a:b_], in1=GT2[:, :, a:b_], op=ALU.mult))
                            pq.append((n, es, j0, j1, e2))
                        if i >= LAP:
                            pn, pes, pj0, pj1, pev_p = pq[i - LAP]
                            qq = pn % 4
                            if qq == 0:
                                ob = 4 + 2 * (pairc["o"] % 2)
                                pairc["o"] += 1
                                pe.wait(K.bank_free[ob], K.bank_free[ob + 1])
                            pe.wait(pev_p, kv_ready)
                            for r in range(2):
                                for j in range(pj0, pj1 + 1):
                                    cc = (j - pn + 1) * 128
                                    mm = nc.tensor.matmul(ps[:, ob + r, qq * 65:qq * 65 + 65], lhsT=EP8[:, pes + r, cc:cc + 128],
                                                          rhs=VA[:, j, g, :], start=(j == pj0), stop=(j == pj1))
                            pev2 = pe.mark(mm)
                            ep_free[pes] = pev2
                            ep_free[pes + 1] = pev2
                            if qq == 3:
                                for r in range(2):
                                    h = hA + r
                                    hl = 2 * hp + r
                                    ov = ps[:, ob + r, 0:260].rearrange("p (q d) -> p q d", d=65)
                                    dve.wait(pev2, esk_ev)
                                    d1 = dve.mark(nc.vector.tensor_scalar(out=small[:, DEN + 4 * r:DEN + 4 * r + 4], in0=ov[:, :, 64],
                                                                          scalar1=small[:, ESK + h:ESK + h + 1], scalar2=None, op0=ALU.add))
                                    dve.wait(d1)
                                    d2 = dve.mark(nc.vector.reciprocal(out=small[:, DEN + 4 * r:DEN + 4 * r + 4], in_=small[:, DEN + 4 * r:DEN + 4 * r + 4]))
                                    dve.wait(d2, og_free if (hl == 0 and pn == 3) else None)
                                    d3 = dve.mark(nc.vector.tensor_tensor(out=og[:, pn - 3:pn + 1, hl * 64:(hl + 1) * 64], in0=ov[:, :, 0:64],
                                                                          in1=small[:, DEN + 4 * r:DEN + 4 * r + 4].unsqueeze(2).to_broadcast([128, 4, 64]),
                                                                          op=ALU.mult))
                                    K.bank_free[ob + r] = d3
                                    o_last = d3
                    g_free2[gb_] = o_last
                    g_free = o_last
                st["qt_free"] = o_last
                sp.wait(o_last)
                ogs = Slot(K, f"og{l}_{g}")
                og_free = ogs.dma(sp, obuf.rearrange("(n p) c -> p n c", p=128)[:, :, g * 512:(g + 1) * 512], og)
                st["o_written"] = og_free
            rt_free[0] = og_free
            rt_free[1] = og_free
        else:
            slopes = alibi_slopes(8)
            scale1 = 128.0 ** -0.5
            lambda_init = 0.8 - 0.6 * math.exp(-0.3 * 1)
            QKh = Rv(32768, 8192, 2048)
            qkg = R[:, 45312:46336].bitcast(F32)
            ogh = rtb[:, :, :].rearrange("p a b -> p (a b)").bitcast(BF16)[:, 0:4096].rearrange("p (t c) -> p t c", c=256)
            o1 = pcs[:, 0:2, :].rearrange("p a b -> p (a b)").rearrange("p (q d) -> p q d", d=256)
            afree = st.get("actT_free")
            lv = vecs[:, V_LAM:V_LAM + 512].rearrange("p (a b d) -> p a b d", b=2, d=128)
            dve.wait(cev, sc_free[0], sc_free[1])
            l1e = dve.mark(nc.vector.tensor_tensor(out=scs[:, 0, 0:256].rearrange("p (a d) -> p a d", d=128), in0=lv[:, :, 0, :],
                                                   in1=lv[:, :, 1, :], op=ALU.mult))
            dve.wait(l1e)
            l2e = dve.mark(nc.vector.tensor_reduce(out=small[:, LAM:LAM + 2], in_=scs[:, 0, 0:256].rearrange("p (a d) -> p a d", d=128),
                                                   axis=AX.X, op=ALU.add))
            act.wait(l2e)
            l3e = act.mark(nc.scalar.activation(out=small[:, LAM:LAM + 2], in_=small[:, LAM:LAM + 2], func=AF.Exp))
            dve.wait(l3e)
            l4e = dve.mark(nc.vector.tensor_tensor(out=small[:, LAM + 2:LAM + 3], in0=small[:, LAM:LAM + 1], in1=small[:, LAM + 1:LAM + 2],
                                                   op=ALU.subtract))
            dve.wait(l4e)
            lam_ev = dve.mark(nc.vector.tensor_scalar(out=small[:, LAM + 3:LAM + 4], in0=small[:, LAM + 2:LAM + 3], scalar1=-1.0,
                                                      scalar2=-lambda_init, op0=ALU.mult, op1=ALU.add))
            dve.wait(afree)
            for i in range(4):
                off = V_BQ if i < 2 else V_BK
                gq_ev = dve.mark(nc.vector.tensor_copy(out=qkg[:, i * 128:(i + 1) * 128], in_=vecs[:, off:off + 128]))
            ones_ev = dve.mark(nc.vector.memset(VB[:, :, 256:257], 1.0))
            og_free = None
            g_free = None
            qk_free = None
            v_free = None
            sb_i = 0
            o_last = None
            GTbufs = [R[:, 53312:53312 + GW], R[:, 46336:46336 + GW]]
            g_frees = [None, None]
            gl = {}

            def g_dma(hh):
                sp.wait(g_frees[hh % 2], afree)
                gl[hh] = Slot(K, f"g{hh}").dma(sp, GTbufs[hh % 2].bitcast(U16), distB_d)
            for h in range(8):
                GTh = GTbufs[h % 2]
                GThu = GTh.bitcast(U16)
                if h == 0:
                    g_dma(0)
                gl_ev = gl[h]
                g_ev = None
                for i in range(4):
                    act.wait(gl_ev)
                    g_ev = act.mark(nc.scalar.activation(out=GTh[:, i * 992:(i + 1) * 992], in_=GThu[:, i * 992:(i + 1) * 992],
                                                         func=AF.Exp, scale=-slopes[h]))
                wv, wev = W.acquire(f"b_h{h}")
                last_pe = None
                qk_ready = []
                pend = None
                for tt in range(NT):
                    if h == 0:
                        proA.need(tt)
                        proA.step()
                    bk = K.next_bank()
                    pe.wait(K.bank_free[bk], wev, proA.ready[tt])
                    for kc in range(DC):
                        mm = nc.tensor.matmul(ps[:, bk, :], lhsT=hT[:, kc, tt * 128:(tt + 1) * 128], rhs=wv[:, kc, 0:512],
                                              start=(kc == 0), stop=(kc == DC - 1))
                    bkv = K.next_bank()
                    pe.wait(K.bank_free[bkv])
                    for kc in range(DC):
                        mm = nc.tensor.matmul(ps[:, bkv, 0:256], lhsT=hT[:, kc, tt * 128:(tt + 1) * 128], rhs=wv[:, kc, 512:768],
                                              start=(kc == 0), stop=(kc == DC - 1))
                    pev = pe.mark(mm)
                    last_pe = pev
                    if pend is not None:
                        pend()
                    s = st["sc_i"] % 2
                    st["sc_i"] += 1
                    sc = scs[:, s, :]
                    act.wait(pev, sc_free[s], l2e)
                    e1 = act.mark(nc.scalar.activation(out=sc, in_=ps[:, bk, :], func=AF.Square))
                    act.wait(v_free, ones_ev)
                    e1v = act.mark(nc.scalar.copy(out=VB[:, tt, 0:256], in_=ps[:, bkv, 0:256]))
                    K.bank_free[bkv] = e1v
                    qk_ready.append(e1v)
                    dve.wait(e1)
                    e2 = dve.mark(nc.vector.tensor_reduce(out=small[:, S8:S8 + 4], in_=sc.rearrange("p (h d) -> p h d", d=128),
                                                          axis=AX.X, op=ALU.add))
                    e4 = rstd_op(small[:, R8:R8 + 4], small[:, S8:S8 + 4], 1.0 / 128, e2)
                    dve.wait(e4)
                    e5 = dve.mark(nc.vector.tensor_tensor(out=sc.rearrange("p (h d) -> p h d", d=128),
                                                          in0=ps[:, bk, :].rearrange("p (h d) -> p h d", d=128),
                                                          in1=small[:, R8:R8 + 4].unsqueeze(2).to_broadcast([128, 4, 128]),
                                                          op=ALU.mult))
                    K.bank_free[bk] = e5
                    dve.wait(e5, qb_free[s], gq_ev)
                    qb = qbs[:, s, :]
                    e6 = dve.mark(nc.vector.tensor_tensor(out=qb, in0=sc, in1=qkg, op=ALU.mult))
                    sc_free[s] = e6

                    def fin(tt=tt, s=s, qb=qb, e6=e6):
                        bk2 = K.next_bank()
                        pe.wait(K.bank_free[bk2], e6, cev2)
                        pb = K.bank_bf(bk2)
                        for j in range(4):
                            mm2 = nc.tensor.transpose(out=pb[:, j * 128:(j + 1) * 128], in_=qb[:, j * 128:(j + 1) * 128], identity=identB[:])
                        pev2 = pe.mark(mm2)
                        qb_free[s] = pev2
                        q = evac_engine()
                        q.wait(pev2, qk_free)
                        e7 = q.mark(copy_on(q, QKh[:, :, tt * 128:(tt + 1) * 128], pb[:, 0:512].rearrange("p (c t) -> p c t", t=128)))
                        K.bank_free[bk2] = e7
                        qk_ready.append(e7)
                    pend = fin
                pend()
                if h == 0:
                    proA.emit_rest()
                W.release(last_pe)
                if h + 1 < 8:
                    g_dma(h + 1)
                steps = [(qb_, c, j) for qb_ in range(4) for c in range(2) for j in range(NT)]
                pq = []
                last_s_pe = None
                pev2 = None
                LA1 = 7
                for i in range(len(steps) + LA1):
                    if i < len(steps):
                        qb_, c, j = steps[i]
                        bk = 4 + (sb_i % 4)
                        sb_i += 1
                        pe.wait(K.bank_free[bk], qk_ready)
                        mm = nc.tensor.matmul(ps[:, bk, :], lhsT=QKh[:, 2 + c, j * 128:(j + 1) * 128],
                                              rhs=QKh[:, c, qb_ * 512:(qb_ + 1) * 512], start=True, stop=True)
                        pev = pe.mark(mm)
                        last_s_pe = pev
                        es = epi["i"] % NEP
                        epi["i"] += 1
                        act.wait(pev, ep_free[es])
                        e1 = act.mark(nc.scalar.activation(out=EP8[:, es, :], in_=ps[:, bk, :], func=AF.Exp, scale=scale1))
                        K.bank_free[bk] = e1
                        off = qb_ * 512 - j * 128 + GOFF
                        dve.wait(e1, g_ev)
                        e2 = dve.mark(nc.vector.tensor_tensor(out=EP8[:, es, :], in0=EP8[:, es, :], in1=GTh[:, off:off + 512], op=ALU.mult))
                        pq.append((qb_, c, j, es, e2))
                    if i >= LA1:
                        qb_, c, j, es, e2 = pq[i - LA1]
                        pe.wait(e2, qk_ready)
                        for qq in range(4):
                            if j == 0:
                                pe.wait(K.bank_free[qq])
                            mm = nc.tensor.matmul(ps[:, qq, 0:257], lhsT=EP8[:, es, qq * 128:(qq + 1) * 128], rhs=VB[:, j, :],
                                                  start=(j == 0), stop=(j == NT - 1))
                        pev2 = pe.mark(mm)
                        ep_free[es] = pev2
                        if j == NT - 1:
                            if c == 0:
                                dve.wait(pev2, [pc_free[k_] for k_ in range(4)], o_last)
                                d1 = dve.mark(nc.vector.reciprocal(out=small[:, DEN:DEN + 4], in_=ps[:, 0:4, 256]))
                                dve.wait(d1)
                                for qq in range(4):
                                    d2 = dve.mark(nc.vector.tensor_scalar(out=o1[:, qq, :], in0=ps[:, qq, 0:256],
                                                                          scalar1=small[:, DEN + qq:DEN + qq + 1], scalar2=None, op0=ALU.mult))
                                    K.bank_free[qq] = d2
                            else:
                                dve.wait(pev2)
                                d1 = dve.mark(nc.vector.reciprocal(out=small[:, DEN + 4:DEN + 8], in_=ps[:, 0:4, 256]))
                                dve.wait(d1, lam_ev)
                                d2 = dve.mark(nc.vector.tensor_scalar(out=small[:, DEN + 4:DEN + 8], in0=small[:, DEN + 4:DEN + 8],
                                                                      scalar1=small[:, LAM + 3:LAM + 4], scalar2=None, op0=ALU.mult))
                                dve.wait(d2)
                                for qq in range(4):
                                    d4 = dve.mark(nc.vector.scalar_tensor_tensor(out=o1[:, qq, :], in0=ps[:, qq, 0:256],
                                                                                 scalar=small[:, DEN + 4 + qq:DEN + 5 + qq], in1=o1[:, qq, :],
                                                                                 op0=ALU.mult, op1=ALU.add))
                                    K.bank_free[qq] = d4
                                a_ev = None
                                for qq in range(4):
                                    act.wait(d4, junk_free[0])
                                    a_ev = act.mark(nc.scalar.activation(out=junk[:, 0:256], in_=o1[:, qq, :], func=AF.Square,
                                                                         accum_out=small[:, S8 + 8 + qq:S8 + 9 + qq]))
                                    junk_free[0] = a_ev
                                act.wait(a_ev)
                                r_ev = act.mark(nc.scalar.activation(out=small[:, R8 + 8:R8 + 12], in_=small[:, S8 + 8:S8 + 12],
                                                                     func=AF.Ln, scale=1.0 / 256, bias=epsc[:, 0:1]))
                                act.wait(r_ev)
                                r_ev = act.mark(nc.scalar.activation(out=small[:, R8 + 8:R8 + 12], in_=small[:, R8 + 8:R8 + 12],
                                                                     func=AF.Exp, scale=-0.5))
                                dve.wait(r_ev)
                                d5 = dve.mark(nc.vector.tensor_tensor(out=o1, in0=o1,
                                                                      in1=small[:, R8 + 8:R8 + 12].unsqueeze(2).to_broadcast([128, 4, 256]),
                                                                      op=ALU.mult))
                                dve.wait(d5, og_free if qb_ == 0 else None, rt_free[0] if (h == 0 and qb_ == 0) else None,
                                         rt_free[1] if (h == 0 and qb_ == 0) else None)
                                d6 = dve.mark(nc.vector.scalar_tensor_tensor(
                                    out=ogh[:, qb_ * 4:(qb_ + 1) * 4, :], in0=o1, scalar=1.0 - lambda_init,
                                    in1=vecs[:, V_SUB:V_SUB + 256].unsqueeze(1).to_broadcast([128, 4, 256]),
                                    op0=ALU.mult, op1=ALU.mult))
                                o_last = d6
                g_frees[h % 2] = o_last
                qk_free = last_s_pe
                v_free = pev2
                sp.wait(o_last)
                ogs = Slot(K, f"og{l}_{h}")
                og_free = ogs.dma(sp, obuf.rearrange("(n p) c -> p n c", p=128)[:, :, h * 256:(h + 1) * 256], ogh)
                st["o_written"] = og_free
            rt_free[0] = og_free
            rt_free[1] = og_free
            for k_ in range(4):
                pc_free[k_] = o_last
        if stop == f"att{l}":
            break
        oT = hT
        proO = Pro(obuf, list(range(NT)), None, oT, 0, st.get("hT_free"), bf16_src=True, src_wait=st["o_written"], emit_all=False, dist=1)
        proO.step()
        proO.step()
        proF[0] = Pro(hres, list(range(0, 8)), 2 + l, hTh, 0, st.get("hTh_free"), emit_all=False, dist=4)
        rp = ResPieces(src, hres, 512, [(tt, b * 512) for b in range(4) for tt in range(NT)])
        for b in range(4):
            wv, wev = W.acquire(f"wo{l}_{b}")
            last_pe = None
            for tt in range(NT):
                if b == 0:
                    proO.need(tt)
                    proO.step()
                bk = K.next_bank()
                pe.wait(K.bank_free[bk], wev, proO.ready[tt])
                for kc in range(DC):
                    mm = nc.tensor.matmul(ps[:, bk, :], lhsT=oT[:, kc, tt * 128:(tt + 1) * 128], rhs=wv[:, kc, :],
                                          start=(kc == 0), stop=(kc == DC - 1))
                pev = pe.mark(mm)
                last_pe = pev
                K.bank_free[bk] = rp.finish(b * NT + tt, ps[:, bk, :], pev)
                if b == 3:
                    proF[0].step(max_tile=tt)
            W.release(last_pe)
        st["hT_free"] = last_pe
        if stop_here(f"wo{l}"):
            break

        for hf in range(2):
            tiles = list(range(hf * 8, hf * 8 + 8))
            tok0 = hf * 1024
            pf = proF[hf]
            pf.emit_rest()
            last_pe = None
            e2 = None
            for blk in range(22):
                wv, wev = W.acquire(f"fi{l}_{hf}_{blk}")
                for tb in range(2):
                    for m in range(2):
                        bg = K.next_bank()
                        pe.wait(K.bank_free[bg], wev, pf.rdy(tiles[tb * 4:(tb + 1) * 4]))
                        for kc in range(DC):
                            mm = nc.tensor.matmul(ps[:, bg, :], lhsT=wv[:, kc, m * 128:(m + 1) * 128],
                                                  rhs=hTh[:, kc, tb * 512:(tb + 1) * 512], start=(kc == 0), stop=(kc == DC - 1))
                        bu = K.next_bank()
                        pe.wait(K.bank_free[bu])
                        for kc in range(DC):
                            mm = nc.tensor.matmul(ps[:, bu, :], lhsT=wv[:, kc, 256 + m * 128:256 + (m + 1) * 128],
                                                  rhs=hTh[:, kc, tb * 512:(tb + 1) * 512], start=(kc == 0), stop=(kc == DC - 1))
                        pev = pe.mark(mm)
                        last_pe = pev
                        s = st["sc_i"] % 2
                        st["sc_i"] += 1
                        act.wait(pev, sc_free[s])
                        e1 = act.mark(nc.scalar.activation(out=scs[:, s, :], in_=ps[:, bg, :], func=AF.Silu))
                        K.bank_free[bg] = e1
                        dve.wait(e1, st.get("actT_free"))
                        e2 = dve.mark(nc.vector.tensor_tensor(out=actT[:, blk * 2 + m, tb * 512:(tb + 1) * 512], in0=ps[:, bu, :],
                                                              in1=scs[:, s, :], op=ALU.mult))
                        K.bank_free[bu] = e2
                        sc_free[s] = e2
                W.release(last_pe)
            st["hTh_free"] = last_pe
            act_ready = e2
            if hf == 1:
                hTa = hTh
                proGa = Pro(hres, list(range(0, 8)), 4 + l, hTa, 0, st["hTh_free"], emit_all=False, dist=3)
            if hf == 0:
                proF[1] = Pro(hres, list(range(8, 16)), 2 + l, hTh, 1024, st["hTh_free"], emit_all=False, dist=3)
            rp = ResPieces(hres, hres, 256, [(tt, b * 256) for b in range(8) for tt in tiles])
            for b in range(8):
                wv, wev = W.acquire(f"fo{l}_{hf}_{b}")
                for ti, tt in enumerate(tiles):
                    if hf == 0 and b >= 5:
                        proF[1].step()
                    if hf == 1:
                        proGa.step()
                    bk = K.next_bank()
                    pe.wait(K.bank_free[bk], wev, act_ready)
                    for kc in range(FC):
                        mm = nc.tensor.matmul(ps[:, bk, 0:256], lhsT=actT[:, kc, ti * 128:(ti + 1) * 128], rhs=wv[:, kc, :],
                                              start=(kc == 0), stop=(kc == FC - 1))
                    pev = pe.mark(mm)
                    last_pe = pev
                    K.bank_free[bk] = rp.finish(b * 8 + ti, ps[:, bk, 0:256], pev)
                W.release(last_pe)
            st["actT_free"] = last_pe
        if stop_here(f"ffn{l}"):
            break

        def stage(sidx, src_ap):
            sp.wait(rt_free[sidx])
            return rt_slot[sidx].dma(sp, rtb[:, sidx, :], src_ap)

        afree = st["actT_free"]
        pls = Slot(K, f"plew{l}")
        pool.wait(afree)
        pls.dma(pool, wpj[:, 0, :], ple_w_proj[l, 0:128, :])
        wpj_ev = pls.dma(pool, wpj[:, 1, :], ple_w_proj[l, 128:256, :])
        ev0 = stage(0, pT[l, 0:128, :])
        ev1 = stage(1, pT[l, 128:256, :])
        act.wait(ev0, afree)
        ea = act.mark(nc.scalar.copy(out=pTs[:, 0, :], in_=rtb[:, 0, :]))
        rt_free[0] = ea
        dve.wait(ev1, afree)
        eb = dve.mark(nc.vector.tensor_copy(out=pTs[:, 1, :], in_=rtb[:, 1, :]))
        rt_free[1] = eb
        pT_ready = [ea, eb]
        ev0 = stage(0, postg_d[l].partition_broadcast(128))
        wp_ready = []
        wpj_ready = [wpj_ev]
        for kc in range(2):
            ev1 = stage(1, ple_w_proj[l, kc * 128:(kc + 1) * 128, :])
            dve.wait(ev1, ev0, afree)
            eb = dve.mark(nc.vector.tensor_tensor(out=wpj2[:, kc, :], in0=rtb[:, 1, :], in1=rtb[:, 0, :], op=ALU.mult))
            rt_free[1] = eb
            wp_ready.append(eb)
        rt_free[0] = eb
        proGa.emit_rest()
        hTb = Rv(0, 16384, 1024)
        proGb = Pro(hres, list(range(8, 16)), 4 + l, hTb, 1024, afree, emit_all=False, dist=3)
        e1 = None
        for tt in range(NT):
            for b in range(4):
                bk = K.next_bank()
                pe.wait(K.bank_free[bk], pT_ready, wpj_ready)
                for kc in range(2):
                    mm = nc.tensor.matmul(ps[:, bk, :], lhsT=pTs[:, kc, tt * 128:(tt + 1) * 128], rhs=wpj[:, kc, b * 512:(b + 1) * 512],
                                          start=(kc == 0), stop=(kc == 1))
                pev = pe.mark(mm)
                act.wait(pev, junk_free[0])
                e1 = act.mark(nc.scalar.activation(out=junk[:, 0:512], in_=ps[:, bk, :], func=AF.Square,
                                                   accum_out=small[:, SP4 + tt * 4 + b:SP4 + tt * 4 + b + 1]))
                junk_free[0] = e1
                K.bank_free[bk] = e1
            if tt % 4 == 3 and tt >= 7:
                proGb.step()
        dve.wait(e1)
        e2 = dve.mark(nc.vector.tensor_reduce(out=small[:, RP:RP + 16], in_=small[:, SP4:SP4 + 64].rearrange("p (t b) -> p t b", b=4),
                                              axis=AX.X, op=ALU.add))
        rp_ev = rstd_op(small[:, RP:RP + 16], small[:, RP:RP + 16], 1.0 / D, e2)
        dst = out if l == 1 else hres
        rp = ResPieces(hres, dst, 512, [(tt, b * 512) for b in range(4) for tt in range(NT)])
        for b in range(4):
            wv, wev = W.acquire(f"wg{l}_{b}")
            last_pe = None
            for tt in range(NT):
                if b == 3 and tt == 12 and l == 0 and stop is None:
                    st["proA_next"] = Pro(hres, list(range(NT)), 1, hT, 0, None, emit_all=False, dist=1)
                    st["proA_next"].step()
                if b == 0:
                    if tt >= 8:
                        proGb.need(tt)
                    proGb.step()
                hsrc, tloc, rdy = (hTa, tt, proGa.ready[tt]) if tt < 8 else (hTb, tt - 8, proGb.ready[tt])
                bg = K.next_bank()
                pe.wait(K.bank_free[bg], wev, rdy)
                for kc in range(DC):
                    mm = nc.tensor.matmul(ps[:, bg, :], lhsT=hsrc[:, kc, tloc * 128:(tloc + 1) * 128], rhs=wv[:, kc, :],
                                          start=(kc == 0), stop=(kc == DC - 1))
                bp = K.next_bank()
                pe.wait(K.bank_free[bp], wp_ready)
                for kc in range(2):
                    mm = nc.tensor.matmul(ps[:, bp, :], lhsT=pTs[:, kc, tt * 128:(tt + 1) * 128], rhs=wpj2[:, kc, b * 512:(b + 1) * 512],
                                          start=(kc == 0), stop=(kc == 1))
                pev = pe.mark(mm)
                last_pe = pev
                s = st["sc_i"] % 2
                st["sc_i"] += 1
                act.wait(pev, sc_free[s])
                e1 = act.mark(nc.scalar.activation(out=scs[:, s, :], in_=ps[:, bg, :], func=AF.Sigmoid))
                K.bank_free[bg] = e1
                dve.wait(e1, rp_ev)
                e2 = dve.mark(nc.vector.scalar_tensor_tensor(out=scs[:, s, :], in0=ps[:, bp, :], scalar=small[:, RP + tt:RP + tt + 1],
                                                             in1=scs[:, s, :], op0=ALU.mult, op1=ALU.mult))
                K.bank_free[bp] = e2
                sc_free[s] = rp.finish(b * NT + tt, scs[:, s, :], e2)
            W.release(last_pe)
        st["hT_free"] = last_pe
        st["hTh_free"] = last_pe
        if stop_here(f"ple{l}"):
            break

    sp.wait([ev for row in res_ev for ev in row])
    K.es.close()
    return nc


def _host_inputs(inputs):
    f = np.float32
    x = np.asarray(inputs["x"], dtype=f)
    p = np.asarray(inputs["p"], dtype=f)
    common = {}
    common["a_w_qkv"] = np.ascontiguousarray(inputs["a_w_qkv"][0], dtype=f)
    common["a_w_o"] = np.ascontiguousarray(inputs["a_w_o"][0], dtype=f)
    common["b_w_qkv"] = np.ascontiguousarray(inputs["b_w_qkv"][0], dtype=f)
    common["b_w_o"] = np.ascontiguousarray(inputs["b_w_o"][0], dtype=f)
    common["w_ffn_in"] = np.ascontiguousarray(inputs["w_ffn_in"], dtype=f)
    common["w_ffn_out"] = np.ascontiguousarray(inputs["w_ffn_out"], dtype=f)
    common["ple_w_proj"] = np.ascontiguousarray(inputs["ple_w_proj"], dtype=f)
    common["ple_w_gate"] = np.ascontiguousarray(inputs["ple_w_gate"], dtype=f)
    gl = []
    for name in ("attn_norm", "ffn_norm", "ple_gate_norm"):
        for i in range(2):
            gl.append(np.asarray(inputs[name][i], dtype=f).reshape(16, 128).T)
    common["gcols"] = np.ascontiguousarray(np.concatenate(gl, axis=1), dtype=f)
    common["postg"] = np.ascontiguousarray(inputs["ple_post_norm"], dtype=f)
    common["vecs"] = np.ascontiguousarray(np.concatenate([
        np.asarray(inputs["a_q_norm"][0]).ravel(), np.asarray(inputs["a_k_norm"][0]).ravel(),
        np.asarray(inputs["a_sink"][0]).ravel(), np.asarray(inputs["b_q_norm"][0]).ravel(),
        np.asarray(inputs["b_k_norm"][0]).ravel(), np.asarray(inputs["b_lambda"][0]).ravel(),
        np.asarray(inputs["b_subln"][0]).ravel()]), dtype=f)
    common["ident"] = np.eye(128, dtype=f)
    kk = np.arange(128)[:, None]
    jj = np.arange(384)[None, :]
    dA = np.abs(jj - 128 - kk)
    jr = jj // 128
    qq = jj % 128
    dA = np.abs(qq - kk + (1 - jr) * 128)
    dA = np.where(dA <= 128, dA, 60000)
    common["distA"] = dA.astype(np.uint16)
    j2 = np.arange(GW)[None, :]
    common["distB"] = np.abs(j2 - kk - GOFF).astype(np.uint16)
    maps = []
    for b in range(8):
        m = dict(common)
        m["x"] = np.ascontiguousarray(x[b])
        m["pT"] = np.ascontiguousarray(np.stack([p[0, b].T, p[1, b].T]), dtype=f)
        maps.append(m)
    return maps


_CACHE = {}


def kernel(**inputs):
    stop = os.environ.get("KSTOP") or None
    ncores = int(os.environ.get("KCORES", "8"))
    maps = _host_inputs(inputs)[:ncores]
    if stop not in _CACHE:
        _CACHE[stop] = build(stop)
    nc = _CACHE[stop]
    res = run_bass_kernel_spmd(nc, maps, core_ids=list(range(ncores)))
    outs = [r["out"] for r in res.results]
    while len(outs) < 8:
        outs.append(np.zeros_like(outs[0]))
    return np.stack(outs, axis=0).astype(np.float32)
```

```python
import math
import os
from contextlib import ExitStack

import numpy as np
import concourse.bass as bass
import concourse.mybir as mybir
from concourse.bass_utils import run_bass_kernel_spmd

F32 = mybir.dt.float32
BF16 = mybir.dt.bfloat16
U16 = mybir.dt.uint16
AF = mybir.ActivationFunctionType
ALU = mybir.AluOpType
AX = mybir.AxisListType

T = 2048
D = 2048
NT = 16
DC = 16
DFF = 5632
FC = 44
EPS = 1e-6
GOFF = 1920
GW = 3968
WEL = 12288
REL = 61440


class Q:
    def __init__(self, K, eng, name):
        self.K = K
        self.eng = eng
        self.sem = K.sem("q_" + name)
        self.cnt = 0
        self.seen = {}

    def wait(self, *evs):
        for ev in evs:
            if ev is None:
                continue
            if isinstance(ev, list):
                self.wait(*ev)
                continue
            sem, val = ev
            k = sem.num
            if self.seen.get(k, 0) >= val:
                continue
            self.eng.wait_ge(sem, val)
            self.seen[k] = val

    def mark(self, ins):
        self.cnt += 1
        ins.then_inc(self.sem, 1)
        return (self.sem, self.cnt)


class Slot:
    def __init__(self, K, name):
        self.sem = K.sem("d_" + name)
        self.cnt = 0

    def dma(self, q, out, in_, **kw):
        self.cnt += 16
        q.eng.dma_start(out=out, in_=in_, **kw).then_inc(self.sem, 16)
        return (self.sem, self.cnt)


class Kern:
    def __init__(self):
        self.nc = bass.Bass("TRN2", target_bir_lowering=False)
        self.es = ExitStack()
        nc = self.nc
        self.pe = Q(self, nc.tensor, "pe")
        self.act = Q(self, nc.scalar, "act")
        self.dve = Q(self, nc.vector, "dve")
        self.pool = Q(self, nc.gpsimd, "pool")
        self.sp = Q(self, nc.sync, "sp")
        self.ps = self.es.enter_context(nc.psum_tensor("ps", [128, 8, 512], F32))
        self.bank_free = [None] * 8
        self.bank_i = 0

    def sem(self, name):
        return self.es.enter_context(self.nc.semaphore(name))

    def sb(self, name, shape, dt):
        return self.es.enter_context(self.nc.sbuf_tensor(name, shape, dt))

    def next_bank(self, lo=0, hi=8):
        n = hi - lo
        b = lo + (self.bank_i % n)
        self.bank_i += 1
        return b

    def bank(self, b):
        return self.ps[:, b, :]

    def bank_bf(self, b):
        return self.ps[:, b, :].bitcast(BF16)


class WStream:
    def __init__(self, K):
        self.K = K
        self.buf = [K.sb(f"wbuf{i}", [128, WEL], BF16) for i in range(2)]
        self.slot = [Slot(K, f"w{i}") for i in range(2)]
        self.free = [None, None]
        self.plan = []
        self.issued = 0
        self.used = 0
        self.evs = []

    def add(self, tag, parts, kc, nb):
        assert kc * nb <= WEL
        self.plan.append((tag, parts, kc, nb))

    def view(self, i):
        _, _, kc, nb = self.plan[i]
        return self.buf[i % 2][:, 0:kc * nb].rearrange("p (k n) -> p k n", n=nb)

    def _issue(self, i):
        s = i % 2
        tag, parts, kc, nb = self.plan[i]
        self.K.pool.wait(self.free[s])
        v = self.view(i)
        ev = None
        for (lo, hi, src) in parts:
            ev = self.slot[s].dma(self.K.pool, v[:, :, lo:hi], src)
        self.evs.append(ev)

    def acquire(self, tag):
        i = self.used
        assert self.plan[i][0] == tag, (self.plan[i][0], tag)
        while self.issued < len(self.plan) and self.issued <= i + 1:
            self._issue(self.issued)
            self.issued += 1
        return self.view(i), self.evs[i]

    def release(self, ev):
        self.free[self.used % 2] = ev
        self.used += 1


def wsrc(w, c0, c1):
    return w.rearrange("(kc p) n -> p kc n", p=128)[:, :, c0:c1]


def alibi_slopes(n):
    return [2.0 ** (-8.0 * (i + 1) / n) for i in range(n)]


def build(stop=None):
    K = Kern()
    nc = K.nc
    pe, act, dve, pool, sp = K.pe, K.act, K.dve, K.pool, K.sp
    ps = K.ps

    def din(name, shape, dt=F32):
        return nc.dram_tensor(name, list(shape), dt, kind="ExternalInput").ap()

    x = din("x", [T, D])
    pT = din("pT", [2, 256, T])
    a_w_qkv = din("a_w_qkv", [D, 2560])
    a_w_o = din("a_w_o", [D, D])
    b_w_qkv = din("b_w_qkv", [D, 6144])
    b_w_o = din("b_w_o", [D, D])
    w_ffn_in = din("w_ffn_in", [2, D, 2 * DFF])
    w_ffn_out = din("w_ffn_out", [2, DFF, D])
    ple_w_proj = din("ple_w_proj", [2, 256, D])
    ple_w_gate = din("ple_w_gate", [2, D, D])
    gcols_d = din("gcols", [128, 6 * 16])
    postg_d = din("postg", [2, D])
    vecs_d = din("vecs", [1184])
    ident_d = din("ident", [128, 128])
    distA_d = din("distA", [128, 384], U16)
    distB_d = din("distB", [128, GW], U16)
    out = nc.dram_tensor("out", [T, D], F32, kind="ExternalOutput").ap()
    hres = nc.dram_tensor("hres", [T, D], F32).ap()
    obuf = nc.dram_tensor("obuf", [T, D], BF16).ap()

    R = K.sb("R", [128, REL], BF16)
    rtb = K.sb("rtb", [128, 2, 2048], F32)
    junk = K.sb("junk", [128, 1024], BF16)
    pcs = K.sb("pcs", [128, 4, 512], F32)
    scs = K.sb("scs", [128, 2, 512], F32)
    qbs = K.sb("qbs", [128, 2, 512], BF16)
    identF = K.sb("identF", [128, 128], F32)
    identB = K.sb("identB", [128, 128], BF16)
    gcols = K.sb("gcols_sb", [128, 96], F32)
    vecs = K.sb("vecs_sb", [128, 1184], F32)
    distA = K.sb("distA_sb", [128, 384], U16)
    small = K.sb("small", [128, 256], F32)
    epsc = K.sb("epsc", [128, 8], F32)
    W = WStream(K)

    def Rv(lo, n, inner):
        return R[:, lo:lo + n].rearrange("p (c t) -> p c t", t=inner)

    hT = Rv(0, 32768, 2048)
    QTg = Rv(32768, 8192, 2048)
    KTd = Rv(40960, 8192, 2048)
    VA = R[:, 49152:49152 + 4160].rearrange("p (t h d) -> p t h d", h=4, d=65)
    QTh = Rv(32768, 4096, 2048)
    KTh = Rv(36864, 4096, 2048)
    VB = R[:, 40960:40960 + 4112].rearrange("p (t d) -> p t d", d=257)
    GT = R[:, 53312:53312 + GW]
    GTu = GT.bitcast(U16)
    EP = R[:, 57280:57280 + 2048].rearrange("p (s n) -> p s n", n=512)
    actT = Rv(0, 45056, 1024)
    hTh = Rv(45056, 16384, 1024)
    pTs = Rv(32768, 4096, 2048)
    wpj = Rv(36864, 4096, 2048)
    wpj2 = Rv(40960, 4096, 2048)
    og = rtb[:, :, :].rearrange("p a b -> p (a b)").bitcast(BF16)[:, 0:8192].rearrange("p (t c) -> p t c", c=512)

    SS = 0
    RS = 16
    S8 = 32
    R8 = 48
    ESK = 64
    DEN = 96
    LAM = 104
    RP = 112
    SP4 = 128
    SSB = 192
    V_AQ, V_AK, V_SINK, V_BQ, V_BK, V_LAM, V_SUB = 0, 64, 128, 160, 288, 416, 928

    for l in range(2):
        if l == 0:
            W.add("a_kv", [(0, 256, wsrc(a_w_qkv, 2048, 2304)), (256, 512, wsrc(a_w_qkv, 2304, 2560))], 16, 512)
            for g in range(4):
                W.add(f"a_q{g}", [(0, 512, wsrc(a_w_qkv, g * 512, (g + 1) * 512))], 16, 512)
            wo = a_w_o
        else:
            for h in range(8):
                W.add(f"b_h{h}", [(0, 256, wsrc(b_w_qkv, h * 256, (h + 1) * 256)),
                                  (256, 512, wsrc(b_w_qkv, 2048 + h * 256, 2048 + (h + 1) * 256)),
                                  (512, 768, wsrc(b_w_qkv, 4096 + h * 256, 4096 + (h + 1) * 256))], 16, 768)
            wo = b_w_o
        for b in range(4):
            W.add(f"wo{l}_{b}", [(0, 512, wsrc(wo, b * 512, (b + 1) * 512))], 16, 512)
        for hf in range(2):
            for blk in range(22):
                W.add(f"fi{l}_{hf}_{blk}", [(0, 256, wsrc(w_ffn_in[l], blk * 256, (blk + 1) * 256)),
                                            (256, 512, wsrc(w_ffn_in[l], DFF + blk * 256, DFF + (blk + 1) * 256))], 16, 512)
            for b in range(8):
                W.add(f"fo{l}_{hf}_{b}", [(0, 256, wsrc(w_ffn_out[l], b * 256, (b + 1) * 256))], FC, 256)
        for b in range(4):
            W.add(f"wg{l}_{b}", [(0, 512, wsrc(ple_w_gate[l], b * 512, (b + 1) * 512))], 16, 512)

    cs = Slot(K, "const")
    cev = cs.dma(sp, identF[:], ident_d)
    cev = cs.dma(sp, gcols[:], gcols_d)
    cev = cs.dma(sp, vecs[:], vecs_d.partition_broadcast(128))
    cev = cs.dma(sp, distA[:], distA_d)
    cs2 = Slot(K, "const2")
    cev2 = cs2.dma(pool, identB[:], ident_d)
    dve.wait(cev)
    e = dve.mark(nc.vector.memset(small[:], 0.0))
    dve.wait(e)
    e = dve.mark(nc.vector.memset(epsc[:], EPS))
    act.wait(cev, e)
    esk_ev = act.mark(nc.scalar.activation(out=small[:, ESK:ESK + 32], in_=vecs[:, V_SINK:V_SINK + 32], func=AF.Exp))

    res_ev = [[None] * 8 for _ in range(NT)]
    st = {"rt_i": 0, "pc_i": 0, "sc_i": 0, "ev_i": 0}
    rt_slot = [Slot(K, "rt0"), Slot(K, "rt1")]
    rt_free = [None, None]
    junk_free = [None]
    pc_slot_in = [Slot(K, f"pci{i}") for i in range(4)]
    pc_slot_out = [Slot(K, f"pco{i}") for i in range(4)]
    pc_free = [None] * 4
    sc_free = [None, None]
    qb_free = [None, None]

    def res_src(l):
        return x if l == 0 else hres

    def evac_engine():
        st["ev_i"] += 1
        return act if st["ev_i"] % 2 == 0 else dve

    def rstd_op(dst, src_ap, n_inv, waits):
        act.wait(waits)
        e = act.mark(nc.scalar.activation(out=dst, in_=src_ap, func=AF.Ln, scale=n_inv, bias=epsc[:, 0:1]))
        act.wait(e)
        return act.mark(nc.scalar.activation(out=dst, in_=dst, func=AF.Exp, scale=-0.5))

    def copy_on(q, out_ap, in_ap):
        if q is act:
            return nc.scalar.copy(out=out_ap, in_=in_ap)
        return nc.vector.tensor_copy(out=out_ap, in_=in_ap)

    class Pro:
        def __init__(self, src, tiles, gci, dstT, tok0, dst_free_ev, bf16_src=False, src_wait=None, emit_all=True, dist=3):
            self.src, self.tiles, self.gci, self.dstT, self.tok0 = src, list(tiles), gci, dstT, tok0
            self.dst_free_ev, self.bf16_src, self.src_wait = dst_free_ev, bf16_src, src_wait
            self.ready = {}
            self.todo = list(tiles)
            self.inflight = []
            self.ld = {}
            self.k = 0
            self.dist = dist
            if emit_all:
                self.emit_rest()

        def _force_one(self):
            if self.inflight:
                tt, _ = self.inflight.pop(0)
                self.emit_pe(tt)
            elif self.todo:
                tt = self.todo.pop(0)
                self.emit_load(tt)
                self.emit_pe(tt)

        def emit_rest(self):
            while self.todo or self.inflight:
                self._force_one()

        def emit_next(self, n=1):
            for _ in range(n):
                self._force_one()

        def need(self, tt):
            while tt not in self.ready:
                self._force_one()

        def step(self, max_tile=None):
            self.k += 1
            while self.inflight and self.k - self.inflight[0][1] >= self.dist:
                tt, _ = self.inflight.pop(0)
                self.emit_pe(tt)
            while self.todo and len(self.inflight) < 2 and (max_tile is None or self.todo[0] <= max_tile):
                tt = self.todo.pop(0)
                self.emit_load(tt)
                self.inflight.append((tt, self.k))

        def rdy(self, tts):
            return [self.ready[t] for t in tts]

        def emit_load(self, tt):
            src = self.src
            s = st["rt_i"] % 2
            st["rt_i"] += 1
            sp.wait(rt_free[s], res_ev[tt] if self.src_wait is None else self.src_wait)
            if self.bf16_src:
                rt_bf = rtb[:, s, :].bitcast(BF16)[:, 0:2048]
                ev_ld = rt_slot[s].dma(sp, rt_bf, src[tt * 128:(tt + 1) * 128, :])
                self.ld[tt] = (s, ev_ld, rt_bf)
            else:
                rt = rtb[:, s, :]
                ev_ld = rt_slot[s].dma(sp, rt, src[tt * 128:(tt + 1) * 128, :])
                col = tt
                act.wait(ev_ld, junk_free[0])
                e1 = act.mark(nc.scalar.activation(out=junk[:], in_=rt[:, 0:1024], func=AF.Square,
                                                   accum_out=small[:, SS + col:SS + col + 1]))
                act.wait(e1)
                e1 = act.mark(nc.scalar.activation(out=junk[:], in_=rt[:, 1024:2048], func=AF.Square,
                                                   accum_out=small[:, SSB + col:SSB + col + 1]))
                junk_free[0] = e1
                dve.wait(e1)
                e2 = dve.mark(nc.vector.tensor_tensor(out=small[:, SS + col:SS + col + 1], in0=small[:, SS + col:SS + col + 1],
                                                      in1=small[:, SSB + col:SSB + col + 1], op=ALU.add))
                e3 = rstd_op(small[:, RS + col:RS + col + 1], small[:, SS + col:SS + col + 1], 1.0 / D, e2)
                dve.wait(e3)
                e4 = dve.mark(nc.vector.tensor_scalar(out=rt, in0=rt, scalar1=small[:, RS + col:RS + col + 1],
                                                      scalar2=None, op0=ALU.mult))
                self.ld[tt] = (s, e4, rt)

        def emit_pe(self, tt):
            gci, dstT, tok0, dst_free_ev = self.gci, self.dstT, self.tok0, self.dst_free_ev
            s, ready, rt = self.ld.pop(tt)
            t0 = tt * 128 - tok0
            evs = []
            if self.bf16_src:
                rt_bf = rt
                for cg in range(2):
                    bk = K.next_bank()
                    pe.wait(K.bank_free[bk], ready, cev2)
                    pb = K.bank_bf(bk)
                    for j in range(8):
                        c = cg * 8 + j
                        mm = nc.tensor.transpose(out=pb[:, j * 128:(j + 1) * 128], in_=rt_bf[:, c * 128:(c + 1) * 128],
                                                 identity=identB[:])
                    pev = pe.mark(mm)
                    q = evac_engine()
                    q.wait(pev, dst_free_ev)
                    e5 = q.mark(copy_on(q, dstT[:, cg * 8:(cg + 1) * 8, t0:t0 + 128],
                                        pb.rearrange("p (c t) -> p c t", t=128)))
                    K.bank_free[bk] = e5
                    evs.append(e5)
                rt_free[s] = pev
            else:
                for cg in range(4):
                    bk = K.next_bank()
                    pe.wait(K.bank_free[bk], ready, cev)
                    for j in range(4):
                        c = cg * 4 + j
                        mm = nc.tensor.transpose(out=ps[:, bk, j * 128:(j + 1) * 128], in_=rt[:, c * 128:(c + 1) * 128],
                                                 identity=identF[:])
                    pev = pe.mark(mm)
                    dve.wait(pev, dst_free_ev)
                    gb = gcols[:, gci * 16 + cg * 4:gci * 16 + cg * 4 + 4].unsqueeze(2).to_broadcast([128, 4, 128])
                    e5 = dve.mark(nc.vector.tensor_tensor(out=dstT[:, cg * 4:(cg + 1) * 4, t0:t0 + 128],
                                                          in0=ps[:, bk, :].rearrange("p (c t) -> p c t", t=128),
                                                          in1=gb, op=ALU.mult))
                    K.bank_free[bk] = e5
                    evs.append(e5)
                rt_free[s] = pev
            self.ready[tt] = evs

    class ResPieces:
        def __init__(self, l_src, dst, ncol, order, ahead=2):
            self.l_src, self.dst, self.ncol, self.order, self.ahead = l_src, dst, ncol, order, ahead
            self.loaded = []

        def _load(self, i):
            tt, c0 = self.order[i]
            s = st["pc_i"] % 4
            st["pc_i"] += 1
            pk = [c0 // 256 + k for k in range(self.ncol // 256)]
            sp.wait(pc_free[s], [res_ev[tt][k] for k in pk])
            ev_in = pc_slot_in[s].dma(sp, pcs[:, s, 0:self.ncol], self.l_src[tt * 128:(tt + 1) * 128, c0:c0 + self.ncol])
            self.loaded.append((s, ev_in))

        def finish(self, i, psum_ap, wait_evs):
            while len(self.loaded) < min(len(self.order), i + 1 + self.ahead):
                self._load(len(self.loaded))
            tt, c0 = self.order[i]
            s, ev_in = self.loaded[i]
            pc = pcs[:, s, 0:self.ncol]
            pk = [c0 // 256 + k for k in range(self.ncol // 256)]
            dve.wait(ev_in, wait_evs)
            e = dve.mark(nc.vector.tensor_tensor(out=pc, in0=psum_ap, in1=pc, op=ALU.add))
            sp.wait(e)
            ev_out = pc_slot_out[s].dma(sp, self.dst[tt * 128:(tt + 1) * 128, c0:c0 + self.ncol], pc)
            pc_free[s] = ev_out
            for k in pk:
                res_ev[tt][k] = ev_out
            return e

    def stop_here(tag):
        if stop != tag:
            return False
        allw = [ev for row in res_ev for ev in row]
        sp.wait(allw)
        ds = Slot(K, "dbg")
        e = ds.dma(sp, out, hres)
        sp.wait(e)
        return True

    LA = 3
    NEP = 8
    EP8 = R[:, 57280:57280 + 4096].rearrange("p (s n) -> p s n", n=512)
    ep_free = [None] * NEP
    epi = {"i": 0}
    proF = {}

    for l in range(2):
        src = res_src(l)
        if "proA_next" in st:
            proA = st.pop("proA_next")
            proA.dst_free_ev = st.get("hT_free")
            proA.step()
        else:
            proA = Pro(src, list(range(NT)), 0 + l, hT, 0, st.get("hT_free"), emit_all=False, dist=1)
            proA.step()
            proA.step()
        if l == 0:
            slopes = alibi_slopes(32)
            dve.wait(cev)
            ones_ev = dve.mark(nc.vector.memset(VA[:, :, :, 64:65], 1.0))
            gqk_ev = dve.mark(nc.vector.tensor_tensor(out=vecs[:, V_AK:V_AK + 64], in0=vecs[:, V_AK:V_AK + 64],
                                                      in1=vecs[:, V_AQ:V_AQ + 64], op=ALU.mult))
            wv, wev = W.acquire("a_kv")
            last_pe = None
            kv_ready = []
            pend = None
            for tt in range(NT):
                proA.need(tt)
                proA.step()
                bk = K.next_bank()
                pe.wait(K.bank_free[bk], wev, proA.ready[tt])
                for kc in range(DC):
                    mm = nc.tensor.matmul(ps[:, bk, :], lhsT=hT[:, kc, tt * 128:(tt + 1) * 128], rhs=wv[:, kc, :],
                                          start=(kc == 0), stop=(kc == DC - 1))
                pev = pe.mark(mm)
                last_pe = pev
                if pend is not None:
                    pend()
                s = st["sc_i"] % 2
                st["sc_i"] += 1
                sc = scs[:, s, 0:256]
                act.wait(pev, sc_free[s])
                e1 = act.mark(nc.scalar.activation(out=sc, in_=ps[:, bk, 0:256], func=AF.Square))
                act.wait(ones_ev)
                e1v = act.mark(nc.scalar.copy(out=VA[:, tt, :, 0:64], in_=ps[:, bk, 256:512].rearrange("p (h d) -> p h d", d=64)))
                kv_ready.append(e1v)
                dve.wait(e1)
                e2 = dve.mark(nc.vector.tensor_reduce(out=small[:, S8:S8 + 4], in_=sc.rearrange("p (h d) -> p h d", d=64),
                                                      axis=AX.X, op=ALU.add))
                e4 = rstd_op(small[:, R8:R8 + 4], small[:, S8:S8 + 4], 1.0 / 64, e2)
                dve.wait(e4)
                e5 = dve.mark(nc.vector.tensor_tensor(out=sc.rearrange("p (h d) -> p h d", d=64),
                                                      in0=ps[:, bk, 0:256].rearrange("p (h d) -> p h d", d=64),
                                                      in1=small[:, R8:R8 + 4].unsqueeze(2).to_broadcast([128, 4, 64]),
                                                      op=ALU.mult))
                K.bank_free[bk] = [e5, e1v]
                dve.wait(e5, qb_free[s], gqk_ev)
                qb = qbs[:, s, :]
                e6 = dve.mark(nc.vector.tensor_tensor(
                    out=qb.rearrange("p (h r d) -> p h r d", r=2, d=64),
                    in0=sc.rearrange("p (h d) -> p h d", d=64).unsqueeze(2).to_broadcast([128, 4, 2, 64]),
                    in1=vecs[:, V_AK:V_AK + 64].unsqueeze(1).unsqueeze(1).to_broadcast([128, 4, 2, 64]),
                    op=ALU.mult))
                sc_free[s] = e6

                def fin(tt=tt, s=s, qb=qb, e6=e6):
                    bk2 = K.next_bank()
                    pe.wait(K.bank_free[bk2], e6, cev2)
                    pb = K.bank_bf(bk2)
                    for j in range(4):
                        mm2 = nc.tensor.transpose(out=pb[:, j * 128:(j + 1) * 128], in_=qb[:, j * 128:(j + 1) * 128], identity=identB[:])
                    pev2 = pe.mark(mm2)
                    qb_free[s] = pev2
                    q = evac_engine()
                    q.wait(pev2)
                    e7 = q.mark(copy_on(q, KTd[:, :, tt * 128:(tt + 1) * 128], pb[:, 0:512].rearrange("p (c t) -> p c t", t=128)))
                    K.bank_free[bk2] = e7
                    kv_ready.append(e7)
                pend = fin
            pend()
            proA.emit_rest()
            W.release(last_pe)
            og_free = None
            g_free = None
            pairc = {"s": 0, "o": 0, "g": 0}
            g_free2 = [None, None]
            for g in range(4):
                wv, wev = W.acquire(f"a_q{g}")
                last_pe = None
                q_ready = []
                pend = None
                for tt in range(NT):
                    bk = K.next_bank()
                    pe.wait(K.bank_free[bk], wev, proA.ready[tt])
                    for kc in range(DC):
                        mm = nc.tensor.matmul(ps[:, bk, :], lhsT=hT[:, kc, tt * 128:(tt + 1) * 128], rhs=wv[:, kc, :],
                                              start=(kc == 0), stop=(kc == DC - 1))
                    pev = pe.mark(mm)
                    last_pe = pev
                    if pend is not None:
                        pend()
                    s = st["sc_i"] % 2
                    st["sc_i"] += 1
                    sc = scs[:, s, :]
                    act.wait(pev, sc_free[s])
                    e1 = act.mark(nc.scalar.activation(out=sc, in_=ps[:, bk, :], func=AF.Square))
                    dve.wait(e1)
                    e2 = dve.mark(nc.vector.tensor_reduce(out=small[:, S8:S8 + 8], in_=sc.rearrange("p (h d) -> p h d", d=64),
                                                          axis=AX.X, op=ALU.add))
                    e4 = rstd_op(small[:, R8:R8 + 8], small[:, S8:S8 + 8], 1.0 / 64, e2)
                    dve.wait(e4, qb_free[s])
                    qb = qbs[:, s, :]
                    e6 = dve.mark(nc.vector.tensor_tensor(out=qb.rearrange("p (h d) -> p h d", d=64),
                                                          in0=ps[:, bk, :].rearrange("p (h d) -> p h d", d=64),
                                                          in1=small[:, R8:R8 + 8].unsqueeze(2).to_broadcast([128, 8, 64]),
                                                          op=ALU.mult))
                    K.bank_free[bk] = e6
                    sc_free[s] = e2

                    def fin(tt=tt, s=s, qb=qb, e6=e6):
                        bk2 = K.next_bank()
                        pe.wait(K.bank_free[bk2], e6, cev2)
                        pb = K.bank_bf(bk2)
                        for j in range(4):
                            mm2 = nc.tensor.transpose(out=pb[:, j * 128:(j + 1) * 128], in_=qb[:, j * 128:(j + 1) * 128], identity=identB[:])
                        pev2 = pe.mark(mm2)
                        qb_free[s] = pev2
                        q = evac_engine()
                        q.wait(pev2, st.get("qt_free"))
                        e7 = q.mark(copy_on(q, QTg[:, :, tt * 128:(tt + 1) * 128], pb[:, 0:512].rearrange("p (c t) -> p c t", t=128)))
                        K.bank_free[bk2] = e7
                        q_ready.append(e7)
                    pend = fin
                pend()
                W.release(last_pe)
                for hp in range(4):
                    hA = g * 8 + 2 * hp
                    c = hp
                    gb_ = pairc["g"] % 2
                    pairc["g"] += 1
                    go = gb_ * 768
                    act.wait(cev, g_free2[gb_])
                    g_ev = act.mark(nc.scalar.activation(out=GT[:, go:go + 384], in_=distA[:], func=AF.Exp, scale=-slopes[hA]))
                    g_ev = act.mark(nc.scalar.activation(out=GT[:, go + 384:go + 768], in_=distA[:], func=AF.Exp, scale=-slopes[hA + 1]))
                    GT2 = GT[:, go:go + 768].rearrange("p (r n) -> p r n", n=384)
                    pq = []
                    o_last = None
                    LAP = 3
                    for i in range(NT + LAP):
                        if i < NT:
                            n = i
                            j0, j1 = max(n - 1, 0), min(n + 1, NT - 1)
                            a, b_ = (j0 - n + 1) * 128, (j1 - n + 2) * 128
                            sb = 2 * (pairc["s"] % 2)
                            pairc["s"] += 1
                            pe.wait(K.bank_free[sb], K.bank_free[sb + 1], q_ready, kv_ready)
                            for j in range(j0, j1 + 1):
                                cc = (j - n + 1) * 128
                                for r in range(2):
                                    mm = nc.tensor.matmul(ps[:, sb + r, cc:cc + 128],
                                                          lhsT=KTd[r * 64:(r + 1) * 64, g, j * 128:(j + 1) * 128],
                                                          rhs=QTg[r * 64:(r + 1) * 64, c, n * 128:(n + 1) * 128], start=True, stop=True)
                            pev = pe.mark(mm)
                            es = 2 * (epi["i"] % 4)
                            epi["i"] += 1
                            act.wait(pev, ep_free[es], ep_free[es + 1])
                            e1 = act.mark(nc.scalar.activation(out=EP8[:, es:es + 2, a:b_], in_=ps[:, sb:sb + 2, a:b_], func=AF.Exp, scale=0.125))
                            K.bank_free[sb] = e1
                            K.bank_free[sb + 1] = e1
                            dve.wait(e1, g_ev)
                            e2 = dve.mark(nc.vector.tensor_tensor(out=EP8[:, es:es + 2, a:b_], in0=EP8[:, es:es + 2, a:b_], in1=GT2[:, :, a:b_], op=ALU.mult))
                            pq.append((n, es, j0, j1, e2))
                        if i >= LAP:
                            pn, pes, pj0, pj1, pev_p = pq[i - LAP]
                            qq = pn % 4
                            if qq == 0:
                                ob = 4 + 2 * (pairc["o"] % 2)
                                pairc["o"] += 1
                                pe.wait(K.bank_free[ob], K.bank_free[ob + 1])
                            pe.wait(pev_p, kv_ready)
                            for r in range(2):
                                for j in range(pj0, pj1 + 1):
                                    cc = (j - pn + 1) * 128
                                    mm = nc.tensor.matmul(ps[:, ob + r, qq * 65:qq * 65 + 65], lhsT=EP8[:, pes + r, cc:cc + 128],
                                                          rhs=VA[:, j, g, :], start=(j == pj0), stop=(j == pj1))
                            pev2 = pe.mark(mm)
                            ep_free[pes] = pev2
                            ep_free[pes + 1] = pev2
                            if qq == 3:
                                for r in range(2):
                                    h = hA + r
                                    hl = 2 * hp + r
                                    ov = ps[:, ob + r, 0:260].rearrange("p (q d) -> p q d", d=65)
                                    dve.wait(pev2, esk_ev)
                                    d1 = dve.mark(nc.vector.tensor_scalar(out=small[:, DEN + 4 * r:DEN + 4 * r + 4], in0=ov[:, :, 64],
                                                                          scalar1=small[:, ESK + h:ESK + h + 1], scalar2=None, op0=ALU.add))
                                    dve.wait(d1)
                                    d2 = dve.mark(nc.vector.reciprocal(out=small[:, DEN + 4 * r:DEN + 4 * r + 4], in_=small[:, DEN + 4 * r:DEN + 4 * r + 4]))
                                    dve.wait(d2, og_free if (hl == 0 and pn == 3) else None)
                                    d3 = dve.mark(nc.vector.tensor_tensor(out=og[:, pn - 3:pn + 1, hl * 64:(hl + 1) * 64], in0=ov[:, :, 0:64],
                                                                          in1=small[:, DEN + 4 * r:DEN + 4 * r + 4].unsqueeze(2).to_broadcast([128, 4, 64]),
                                                                          op=ALU.mult))
                                    K.bank_free[ob + r] = d3
                                    o_last = d3
                    g_free2[gb_] = o_last
                    g_free = o_last
                st["qt_free"] = o_last
                sp.wait(o_last)
                ogs = Slot(K, f"og{l}_{g}")
                og_free = ogs.dma(sp, obuf.rearrange("(n p) c -> p n c", p=128)[:, :, g * 512:(g + 1) * 512], og)
                st["o_written"] = og_free
            rt_free[0] = og_free
            rt_free[1] = og_free
        else:
            slopes = alibi_slopes(8)
            scale1 = 128.0 ** -0.5
            lambda_init = 0.8 - 0.6 * math.exp(-0.3 * 1)
            QKh = Rv(32768, 8192, 2048)
            qkg = R[:, 45312:46336].bitcast(F32)
            ogh = rtb[:, :, :].rearrange("p a b -> p (a b)").bitcast(BF16)[:, 0:4096].rearrange("p (t c) -> p t c", c=256)
            o1 = pcs[:, 0:2, :].rearrange("p a b -> p (a b)").rearrange("p (q d) -> p q d", d=256)
            afree = st.get("actT_free")
            lv = vecs[:, V_LAM:V_LAM + 512].rearrange("p (a b d) -> p a b d", b=2, d=128)
            dve.wait(cev, sc_free[0], sc_free[1])
            l1e = dve.mark(nc.vector.tensor_tensor(out=scs[:, 0, 0:256].rearrange("p (a d) -> p a d", d=128), in0=lv[:, :, 0, :],
                                                   in1=lv[:, :, 1, :], op=ALU.mult))
            dve.wait(l1e)
            l2e = dve.mark(nc.vector.tensor_reduce(out=small[:, LAM:LAM + 2], in_=scs[:, 0, 0:256].rearrange("p (a d) -> p a d", d=128),
                                                   axis=AX.X, op=ALU.add))
            act.wait(l2e)
            l3e = act.mark(nc.scalar.activation(out=small[:, LAM:LAM + 2], in_=small[:, LAM:LAM + 2], func=AF.Exp))
            dve.wait(l3e)
            l4e = dve.mark(nc.vector.tensor_tensor(out=small[:, LAM + 2:LAM + 3], in0=small[:, LAM:LAM + 1], in1=small[:, LAM + 1:LAM + 2],
                                                   op=ALU.subtract))
            dve.wait(l4e)
            lam_ev = dve.mark(nc.vector.tensor_scalar(out=small[:, LAM + 3:LAM + 4], in0=small[:, LAM + 2:LAM + 3], scalar1=-1.0,
                                                      scalar2=-lambda_init, op0=ALU.mult, op1=ALU.add))
            dve.wait(afree)
            for i in range(4):
                off = V_BQ if i < 2 else V_BK
                gq_ev = dve.mark(nc.vector.tensor_copy(out=qkg[:, i * 128:(i + 1) * 128], in_=vecs[:, off:off + 128]))
            ones_ev = dve.mark(nc.vector.memset(VB[:, :, 256:257], 1.0))
            og_free = None
            g_free = None
            qk_free = None
            v_free = None
            sb_i = 0
            o_last = None
            GTbufs = [R[:, 53312:53312 + GW], R[:, 46336:46336 + GW]]
            g_frees = [None, None]
            gl = {}

            def g_dma(hh):
                sp.wait(g_frees[hh % 2], afree)
                gl[hh] = Slot(K, f"g{hh}").dma(sp, GTbufs[hh % 2].bitcast(U16), distB_d)
            for h in range(8):
                GTh = GTbufs[h % 2]
                GThu = GTh.bitcast(U16)
                if h == 0:
                    g_dma(0)
                gl_ev = gl[h]
                g_ev = None
                for i in range(4):
                    act.wait(gl_ev)
                    g_ev = act.mark(nc.scalar.activation(out=GTh[:, i * 992:(i + 1) * 992], in_=GThu[:, i * 992:(i + 1) * 992],
                                                         func=AF.Exp, scale=-slopes[h]))
                wv, wev = W.acquire(f"b_h{h}")
                last_pe = None
                qk_ready = []
                pend = None
                for tt in range(NT):
                    if h == 0:
                        proA.need(tt)
                        proA.step()
                    bk = K.next_bank()
                    pe.wait(K.bank_free[bk], wev, proA.ready[tt])
                    for kc in range(DC):
                        mm = nc.tensor.matmul(ps[:, bk, :], lhsT=hT[:, kc, tt * 128:(tt + 1) * 128], rhs=wv[:, kc, 0:512],
                                              start=(kc == 0), stop=(kc == DC - 1))
                    bkv = K.next_bank()
                    pe.wait(K.bank_free[bkv])
                    for kc in range(DC):
                        mm = nc.tensor.matmul(ps[:, bkv, 0:256], lhsT=hT[:, kc, tt * 128:(tt + 1) * 128], rhs=wv[:, kc, 512:768],
                                              start=(kc == 0), stop=(kc == DC - 1))
                    pev = pe.mark(mm)
                    last_pe = pev
                    if pend is not None:
                        pend()
                    s = st["sc_i"] % 2
                    st["sc_i"] += 1
                    sc = scs[:, s, :]
                    act.wait(pev, sc_free[s], l2e)
                    e1 = act.mark(nc.scalar.activation(out=sc, in_=ps[:, bk, :], func=AF.Square))
                    act.wait(v_free, ones_ev)
                    e1v = act.mark(nc.scalar.copy(out=VB[:, tt, 0:256], in_=ps[:, bkv, 0:256]))
                    K.bank_free[bkv] = e1v
                    qk_ready.append(e1v)
                    dve.wait(e1)
                    e2 = dve.mark(nc.vector.tensor_reduce(out=small[:, S8:S8 + 4], in_=sc.rearrange("p (h d) -> p h d", d=128),
                                                          axis=AX.X, op=ALU.add))
                    e4 = rstd_op(small[:, R8:R8 + 4], small[:, S8:S8 + 4], 1.0 / 128, e2)
                    dve.wait(e4)
                    e5 = dve.mark(nc.vector.tensor_tensor(out=sc.rearrange("p (h d) -> p h d", d=128),
                                                          in0=ps[:, bk, :].rearrange("p (h d) -> p h d", d=128),
                                                          in1=small[:, R8:R8 + 4].unsqueeze(2).to_broadcast([128, 4, 128]),
                                                          op=ALU.mult))
                    K.bank_free[bk] = e5
                    dve.wait(e5, qb_free[s], gq_ev)
                    qb = qbs[:, s, :]
                    e6 = dve.mark(nc.vector.tensor_tensor(out=qb, in0=sc, in1=qkg, op=ALU.mult))
                    sc_free[s] = e6

                    def fin(tt=tt, s=s, qb=qb, e6=e6):
                        bk2 = K.next_bank()
                        pe.wait(K.bank_free[bk2], e6, cev2)
                        pb = K.bank_bf(bk2)
                        for j in range(4):
                            mm2 = nc.tensor.transpose(out=pb[:, j * 128:(j + 1) * 128], in_=qb[:, j * 128:(j + 1) * 128], identity=identB[:])
                        pev2 = pe.mark(mm2)
                        qb_free[s] = pev2
                        q = evac_engine()
                        q.wait(pev2, qk_free)
                        e7 = q.mark(copy_on(q, QKh[:, :, tt * 128:(tt + 1) * 128], pb[:, 0:512].rearrange("p (c t) -> p c t", t=128)))
                        K.bank_free[bk2] = e7
                        qk_ready.append(e7)
                    pend = fin
                pend()
                if h == 0:
                    proA.emit_rest()
                W.release(last_pe)
                if h + 1 < 8:
                    g_dma(h + 1)
                steps = [(qb_, c, j) for qb_ in range(4) for c in range(2) for j in range(NT)]
                pq = []
                last_s_pe = None
                pev2 = None
                LA1 = 7
                for i in range(len(steps) + LA1):
                    if i < len(steps):
                        qb_, c, j = steps[i]
                        bk = 4 + (sb_i % 4)
                        sb_i += 1
                        pe.wait(K.bank_free[bk], qk_ready)
                        mm = nc.tensor.matmul(ps[:, bk, :], lhsT=QKh[:, 2 + c, j * 128:(j + 1) * 128],
                                              rhs=QKh[:, c, qb_ * 512:(qb_ + 1) * 512], start=True, stop=True)
                        pev = pe.mark(mm)
                        last_s_pe = pev
                        es = epi["i"] % NEP
                        epi["i"] += 1
                        act.wait(pev, ep_free[es])
                        e1 = act.mark(nc.scalar.activation(out=EP8[:, es, :], in_=ps[:, bk, :], func=AF.Exp, scale=scale1))
                        K.bank_free[bk] = e1
                        off = qb_ * 512 - j * 128 + GOFF
                        dve.wait(e1, g_ev)
                        e2 = dve.mark(nc.vector.tensor_tensor(out=EP8[:, es, :], in0=EP8[:, es, :], in1=GTh[:, off:off + 512], op=ALU.mult))
                        pq.append((qb_, c, j, es, e2))
                    if i >= LA1:
                        qb_, c, j, es, e2 = pq[i - LA1]
                        pe.wait(e2, qk_ready)
                        for qq in range(4):
                            if j == 0:
                                pe.wait(K.bank_free[qq])
                            mm = nc.tensor.matmul(ps[:, qq, 0:257], lhsT=EP8[:, es, qq * 128:(qq + 1) * 128], rhs=VB[:, j, :],
                                                  start=(j == 0), stop=(j == NT - 1))
                        pev2 = pe.mark(mm)
                        ep_free[es] = pev2
                        if j == NT - 1:
                            if c == 0:
                                dve.wait(pev2, [pc_free[k_] for k_ in range(4)], o_last)
                                d1 = dve.mark(nc.vector.reciprocal(out=small[:, DEN:DEN + 4], in_=ps[:, 0:4, 256]))
                                dve.wait(d1)
                                for qq in range(4):
                                    d2 = dve.mark(nc.vector.tensor_scalar(out=o1[:, qq, :], in0=ps[:, qq, 0:256],
                                                                          scalar1=small[:, DEN + qq:DEN + qq + 1], scalar2=None, op0=ALU.mult))
                                    K.bank_free[qq] = d2
                            else:
                                dve.wait(pev2)
                                d1 = dve.mark(nc.vector.reciprocal(out=small[:, DEN + 4:DEN + 8], in_=ps[:, 0:4, 256]))
                                dve.wait(d1, lam_ev)
                                d2 = dve.mark(nc.vector.tensor_scalar(out=small[:, DEN + 4:DEN + 8], in0=small[:, DEN + 4:DEN + 8],
                                                                      scalar1=small[:, LAM + 3:LAM + 4], scalar2=None, op0=ALU.mult))
                                dve.wait(d2)
                                for qq in range(4):
                                    d4 = dve.mark(nc.vector.scalar_tensor_tensor(out=o1[:, qq, :], in0=ps[:, qq, 0:256],
                                                                                 scalar=small[:, DEN + 4 + qq:DEN + 5 + qq], in1=o1[:, qq, :],
                                                                                 op0=ALU.mult, op1=ALU.add))
                                    K.bank_free[qq] = d4
                                a_ev = None
                                for qq in range(4):
                                    act.wait(d4, junk_free[0])
                                    a_ev = act.mark(nc.scalar.activation(out=junk[:, 0:256], in_=o1[:, qq, :], func=AF.Square,
                                                                         accum_out=small[:, S8 + 8 + qq:S8 + 9 + qq]))
                                    junk_free[0] = a_ev
                                act.wait(a_ev)
                                r_ev = act.mark(nc.scalar.activation(out=small[:, R8 + 8:R8 + 12], in_=small[:, S8 + 8:S8 + 12],
                                                                     func=AF.Ln, scale=1.0 / 256, bias=epsc[:, 0:1]))
                                act.wait(r_ev)
                                r_ev = act.mark(nc.scalar.activation(out=small[:, R8 + 8:R8 + 12], in_=small[:, R8 + 8:R8 + 12],
                                                                     func=AF.Exp, scale=-0.5))
                                dve.wait(r_ev)
                                d5 = dve.mark(nc.vector.tensor_tensor(out=o1, in0=o1,
                                                                      in1=small[:, R8 + 8:R8 + 12].unsqueeze(2).to_broadcast([128, 4, 256]),
                                                                      op=ALU.mult))
                                dve.wait(d5, og_free if qb_ == 0 else None, rt_free[0] if (h == 0 and qb_ == 0) else None,
                                         rt_free[1] if (h == 0 and qb_ == 0) else None)
                                d6 = dve.mark(nc.vector.scalar_tensor_tensor(
                                    out=ogh[:, qb_ * 4:(qb_ + 1) * 4, :], in0=o1, scalar=1.0 - lambda_init,
                                    in1=vecs[:, V_SUB:V_SUB + 256].unsqueeze(1).to_broadcast([128, 4, 256]),
                                    op0=ALU.mult, op1=ALU.mult))
                                o_last = d6
                g_frees[h % 2] = o_last
                qk_free = last_s_pe
                v_free = pev2
                sp.wait(o_last)
                ogs = Slot(K, f"og{l}_{h}")
                og_free = ogs.dma(sp, obuf.rearrange("(n p) c -> p n c", p=128)[:, :, h * 256:(h + 1) * 256], ogh)
                st["o_written"] = og_free
            rt_free[0] = og_free
            rt_free[1] = og_free
            for k_ in range(4):
                pc_free[k_] = o_last
        if stop == f"att{l}":
            break
        oT = hT
        proO = Pro(obuf, list(range(NT)), None, oT, 0, st.get("hT_free"), bf16_src=True, src_wait=st["o_written"], emit_all=False, dist=1)
        proO.step()
        proO.step()
        proF[0] = Pro(hres, list(range(0, 8)), 2 + l, hTh, 0, st.get("hTh_free"), emit_all=False, dist=4)
        rp = ResPieces(src, hres, 512, [(tt, b * 512) for b in range(4) for tt in range(NT)])
        for b in range(4):
            wv, wev = W.acquire(f"wo{l}_{b}")
            last_pe = None
            for tt in range(NT):
                if b == 0:
                    proO.need(tt)
                    proO.step()
                bk = K.next_bank()
                pe.wait(K.bank_free[bk], wev, proO.ready[tt])
                for kc in range(DC):
                    mm = nc.tensor.matmul(ps[:, bk, :], lhsT=oT[:, kc, tt * 128:(tt + 1) * 128], rhs=wv[:, kc, :],
                                          start=(kc == 0), stop=(kc == DC - 1))
                pev = pe.mark(mm)
                last_pe = pev
                K.bank_free[bk] = rp.finish(b * NT + tt, ps[:, bk, :], pev)
                if b == 3:
                    proF[0].step(max_tile=tt)
            W.release(last_pe)
        st["hT_free"] = last_pe
        if stop_here(f"wo{l}"):
            break

        for hf in range(2):
            tiles = list(range(hf * 8, hf * 8 + 8))
            tok0 = hf * 1024
            pf = proF[hf]
            pf.emit_rest()
            last_pe = None
            e2 = None
            for blk in range(22):
                wv, wev = W.acquire(f"fi{l}_{hf}_{blk}")
                for tb in range(2):
                    for m in range(2):
                        bg = K.next_bank()
                        pe.wait(K.bank_free[bg], wev, pf.rdy(tiles[tb * 4:(tb + 1) * 4]))
                        for kc in range(DC):
                            mm = nc.tensor.matmul(ps[:, bg, :], lhsT=wv[:, kc, m * 128:(m + 1) * 128],
                                                  rhs=hTh[:, kc, tb * 512:(tb + 1) * 512], start=(kc == 0), stop=(kc == DC - 1))
                        bu = K.next_bank()
                        pe.wait(K.bank_free[bu])
                        for kc in range(DC):
                            mm = nc.tensor.matmul(ps[:, bu, :], lhsT=wv[:, kc, 256 + m * 128:256 + (m + 1) * 128],
                                                  rhs=hTh[:, kc, tb * 512:(tb + 1) * 512], start=(kc == 0), stop=(kc == DC - 1))
                        pev = pe.mark(mm)
                        last_pe = pev
                        s = st["sc_i"] % 2
                        st["sc_i"] += 1
                        act.wait(pev, sc_free[s])
                        e1 = act.mark(nc.scalar.activation(out=scs[:, s, :], in_=ps[:, bg, :], func=AF.Silu))
                        K.bank_free[bg] = e1
                        dve.wait(e1, st.get("actT_free"))
                        e2 = dve.mark(nc.vector.tensor_tensor(out=actT[:, blk * 2 + m, tb * 512:(tb + 1) * 512], in0=ps[:, bu, :],
                                                              in1=scs[:, s, :], op=ALU.mult))
                        K.bank_free[bu] = e2
                        sc_free[s] = e2
                W.release(last_pe)
            st["hTh_free"] = last_pe
            act_ready = e2
            if hf == 1:
                hTa = hTh
                proGa = Pro(hres, list(range(0, 8)), 4 + l, hTa, 0, st["hTh_free"], emit_all=False, dist=3)
            if hf == 0:
                proF[1] = Pro(hres, list(range(8, 16)), 2 + l, hTh, 1024, st["hTh_free"], emit_all=False, dist=3)
            rp = ResPieces(hres, hres, 256, [(tt, b * 256) for b in range(8) for tt in tiles])
            for b in range(8):
                wv, wev = W.acquire(f"fo{l}_{hf}_{b}")
                for ti, tt in enumerate(tiles):
                    if hf == 0 and b >= 5:
                        proF[1].step()
                    if hf == 1:
                        proGa.step()
                    bk = K.next_bank()
                    pe.wait(K.bank_free[bk], wev, act_ready)
                    for kc in range(FC):
                        mm = nc.tensor.matmul(ps[:, bk, 0:256], lhsT=actT[:, kc, ti * 128:(ti + 1) * 128], rhs=wv[:, kc, :],
                                              start=(kc == 0), stop=(kc == FC - 1))
                    pev = pe.mark(mm)
                    last_pe = pev
                    K.bank_free[bk] = rp.finish(b * 8 + ti, ps[:, bk, 0:256], pev)
                W.release(last_pe)
            st["actT_free"] = last_pe
        if stop_here(f"ffn{l}"):
            break

        def stage(sidx, src_ap):
            sp.wait(rt_free[sidx])
            return rt_slot[sidx].dma(sp, rtb[:, sidx, :], src_ap)

        afree = st["actT_free"]
        pls = Slot(K, f"plew{l}")
        pool.wait(afree)
        pls.dma(pool, wpj[:, 0, :], ple_w_proj[l, 0:128, :])
        wpj_ev = pls.dma(pool, wpj[:, 1, :], ple_w_proj[l, 128:256, :])
        ev0 = stage(0, pT[l, 0:128, :])
        ev1 = stage(1, pT[l, 128:256, :])
        act.wait(ev0, afree)
        ea = act.mark(nc.scalar.copy(out=pTs[:, 0, :], in_=rtb[:, 0, :]))
        rt_free[0] = ea
        dve.wait(ev1, afree)
        eb = dve.mark(nc.vector.tensor_copy(out=pTs[:, 1, :], in_=rtb[:, 1, :]))
        rt_free[1] = eb
        pT_ready = [ea, eb]
        ev0 = stage(0, postg_d[l].partition_broadcast(128))
        wp_ready = []
        wpj_ready = [wpj_ev]
        for kc in range(2):
            ev1 = stage(1, ple_w_proj[l, kc * 128:(kc + 1) * 128, :])
            dve.wait(ev1, ev0, afree)
            eb = dve.mark(nc.vector.tensor_tensor(out=wpj2[:, kc, :], in0=rtb[:, 1, :], in1=rtb[:, 0, :], op=ALU.mult))
            rt_free[1] = eb
            wp_ready.append(eb)
        rt_free[0] = eb
        proGa.emit_rest()
        hTb = Rv(0, 16384, 1024)
        proGb = Pro(hres, list(range(8, 16)), 4 + l, hTb, 1024, afree, emit_all=False, dist=3)
        e1 = None
        for tt in range(NT):
            for b in range(4):
                bk = K.next_bank()
                pe.wait(K.bank_free[bk], pT_ready, wpj_ready)
                for kc in range(2):
                    mm = nc.tensor.matmul(ps[:, bk, :], lhsT=pTs[:, kc, tt * 128:(tt + 1) * 128], rhs=wpj[:, kc, b * 512:(b + 1) * 512],
                                          start=(kc == 0), stop=(kc == 1))
                pev = pe.mark(mm)
                act.wait(pev, junk_free[0])
                e1 = act.mark(nc.scalar.activation(out=junk[:, 0:512], in_=ps[:, bk, :], func=AF.Square,
                                                   accum_out=small[:, SP4 + tt * 4 + b:SP4 + tt * 4 + b + 1]))
                junk_free[0] = e1
                K.bank_free[bk] = e1
            if tt % 4 == 3 and tt >= 7:
                proGb.step()
        dve.wait(e1)
        e2 = dve.mark(nc.vector.tensor_reduce(out=small[:, RP:RP + 16], in_=small[:, SP4:SP4 + 64].rearrange("p (t b) -> p t b", b=4),
                                              axis=AX.X, op=ALU.add))
        rp_ev = rstd_op(small[:, RP:RP + 16], small[:, RP:RP + 16], 1.0 / D, e2)
        dst = out if l == 1 else hres
        rp = ResPieces(hres, dst, 512, [(tt, b * 512) for b in range(4) for tt in range(NT)])
        for b in range(4):
            wv, wev = W.acquire(f"wg{l}_{b}")
            last_pe = None
            for tt in range(NT):
                if b == 3 and tt == 12 and l == 0 and stop is None:
                    st["proA_next"] = Pro(hres, list(range(NT)), 1, hT, 0, None, emit_all=False, dist=1)
                    st["proA_next"].step()
                if b == 0:
                    if tt >= 8:
                        proGb.need(tt)
                    proGb.step()
                hsrc, tloc, rdy = (hTa, tt, proGa.ready[tt]) if tt < 8 else (hTb, tt - 8, proGb.ready[tt])
                bg = K.next_bank()
                pe.wait(K.bank_free[bg], wev, rdy)
                for kc in range(DC):
                    mm = nc.tensor.matmul(ps[:, bg, :], lhsT=hsrc[:, kc, tloc * 128:(tloc + 1) * 128], rhs=wv[:, kc, :],
                                          start=(kc == 0), stop=(kc == DC - 1))
                bp = K.next_bank()
                pe.wait(K.bank_free[bp], wp_ready)
                for kc in range(2):
                    mm = nc.tensor.matmul(ps[:, bp, :], lhsT=pTs[:, kc, tt * 128:(tt + 1) * 128], rhs=wpj2[:, kc, b * 512:(b + 1) * 512],
                                          start=(kc == 0), stop=(kc == 1))
                pev = pe.mark(mm)
                last_pe = pev
                s = st["sc_i"] % 2
                st["sc_i"] += 1
                act.wait(pev, sc_free[s])
                e1 = act.mark(nc.scalar.activation(out=scs[:, s, :], in_=ps[:, bg, :], func=AF.Sigmoid))
                K.bank_free[bg] = e1
                dve.wait(e1, rp_ev)
                e2 = dve.mark(nc.vector.scalar_tensor_tensor(out=scs[:, s, :], in0=ps[:, bp, :], scalar=small[:, RP + tt:RP + tt + 1],
                                                             in1=scs[:, s, :], op0=ALU.mult, op1=ALU.mult))
                K.bank_free[bp] = e2
                sc_free[s] = rp.finish(b * NT + tt, scs[:, s, :], e2)
            W.release(last_pe)
        st["hT_free"] = last_pe
        st["hTh_free"] = last_pe
        if stop_here(f"ple{l}"):
            break

    sp.wait([ev for row in res_ev for ev in row])
    K.es.close()
    return nc


def _host_inputs(inputs):
    f = np.float32
    x = np.asarray(inputs["x"], dtype=f)
    p = np.asarray(inputs["p"], dtype=f)
    common = {}
    common["a_w_qkv"] = np.ascontiguousarray(inputs["a_w_qkv"][0], dtype=f)
    common["a_w_o"] = np.ascontiguousarray(inputs["a_w_o"][0], dtype=f)
    common["b_w_qkv"] = np.ascontiguousarray(inputs["b_w_qkv"][0], dtype=f)
    common["b_w_o"] = np.ascontiguousarray(inputs["b_w_o"][0], dtype=f)
    common["w_ffn_in"] = np.ascontiguousarray(inputs["w_ffn_in"], dtype=f)
    common["w_ffn_out"] = np.ascontiguousarray(inputs["w_ffn_out"], dtype=f)
    common["ple_w_proj"] = np.ascontiguousarray(inputs["ple_w_proj"], dtype=f)
    common["ple_w_gate"] = np.ascontiguousarray(inputs["ple_w_gate"], dtype=f)
    gl = []
    for name in ("attn_norm", "ffn_norm", "ple_gate_norm"):
        for i in range(2):
            gl.append(np.asarray(inputs[name][i], dtype=f).reshape(16, 128).T)
    common["gcols"] = np.ascontiguousarray(np.concatenate(gl, axis=1), dtype=f)
    common["postg"] = np.ascontiguousarray(inputs["ple_post_norm"], dtype=f)
    common["vecs"] = np.ascontiguousarray(np.concatenate([
        np.asarray(inputs["a_q_norm"][0]).ravel(), np.asarray(inputs["a_k_norm"][0]).ravel(),
        np.asarray(inputs["a_sink"][0]).ravel(), np.asarray(inputs["b_q_norm"][0]).ravel(),
        np.asarray(inputs["b_k_norm"][0]).ravel(), np.asarray(inputs["b_lambda"][0]).ravel(),
        np.asarray(inputs["b_subln"][0]).ravel()]), dtype=f)
    common["ident"] = np.eye(128, dtype=f)
    kk = np.arange(128)[:, None]
    jj = np.arange(384)[None, :]
    dA = np.abs(jj - 128 - kk)
    jr = jj // 128
    qq = jj % 128
    dA = np.abs(qq - kk + (1 - jr) * 128)
    dA = np.where(dA <= 128, dA, 60000)
    common["distA"] = dA.astype(np.uint16)
    j2 = np.arange(GW)[None, :]
    common["distB"] = np.abs(j2 - kk - GOFF).astype(np.uint16)
    maps = []
    for b in range(8):
        m = dict(common)
        m["x"] = np.ascontiguousarray(x[b])
        m["pT"] = np.ascontiguousarray(np.stack([p[0, b].T, p[1, b].T]), dtype=f)
        maps.append(m)
    return maps


_CACHE = {}


def kernel(**inputs):
    stop = os.environ.get("KSTOP") or None
    ncores = int(os.environ.get("KCORES", "8"))
    maps = _host_inputs(inputs)[:ncores]
    if stop not in _CACHE:
        _CACHE[stop] = build(stop)
    nc = _CACHE[stop]
    res = run_bass_kernel_spmd(nc, maps, core_ids=list(range(ncores)))
    outs = [r["out"] for r in res.results]
    while len(outs) < 8:
        outs.append(np.zeros_like(outs[0]))
    return np.stack(outs, axis=0).astype(np.float32)
```

```python
import math
import os
from contextlib import ExitStack

import numpy as np
import concourse.bass as bass
import concourse.mybir as mybir
from concourse.bass_utils import run_bass_kernel_spmd

F32 = mybir.dt.float32
BF16 = mybir.dt.bfloat16
U16 = mybir.dt.uint16
AF = mybir.ActivationFunctionType
ALU = mybir.AluOpType
AX = mybir.AxisListType

T = 2048
D = 2048
NT = 16
DC = 16
DFF = 5632
FC = 44
EPS = 1e-6
GOFF = 1920
GW = 3968
WEL = 12288
REL = 61440


class Q:
    def __init__(self, K, eng, name):
        self.K = K
        self.eng = eng
        self.sem = K.sem("q_" + name)
        self.cnt = 0
        self.seen = {}

    def wait(self, *evs):
        for ev in evs:
            if ev is None:
                continue
            if isinstance(ev, list):
                self.wait(*ev)
                continue
            sem, val = ev
            k = sem.num
            if self.seen.get(k, 0) >= val:
                continue
            self.eng.wait_ge(sem, val)
            self.seen[k] = val

    def mark(self, ins):
        self.cnt += 1
        ins.then_inc(self.sem, 1)
        return (self.sem, self.cnt)


class Slot:
    def __init__(self, K, name):
        self.sem = K.sem("d_" + name)
        self.cnt = 0

    def dma(self, q, out, in_, **kw):
        self.cnt += 16
        q.eng.dma_start(out=out, in_=in_, **kw).then_inc(self.sem, 16)
        return (self.sem, self.cnt)


class Kern:
    def __init__(self):
        self.nc = bass.Bass("TRN2", target_bir_lowering=False)
        self.es = ExitStack()
        nc = self.nc
        self.pe = Q(self, nc.tensor, "pe")
        self.act = Q(self, nc.scalar, "act")
        self.dve = Q(self, nc.vector, "dve")
        self.pool = Q(self, nc.gpsimd, "pool")
        self.sp = Q(self, nc.sync, "sp")
        self.ps = self.es.enter_context(nc.psum_tensor("ps", [128, 8, 512], F32))
        self.bank_free = [None] * 8
        self.bank_i = 0

    def sem(self, name):
        return self.es.enter_context(self.nc.semaphore(name))

    def sb(self, name, shape, dt):
        return self.es.enter_context(self.nc.sbuf_tensor(name, shape, dt))

    def next_bank(self, lo=0, hi=8):
        n = hi - lo
        b = lo + (self.bank_i % n)
        self.bank_i += 1
        return b

    def bank(self, b):
        return self.ps[:, b, :]

    def bank_bf(self, b):
        return self.ps[:, b, :].bitcast(BF16)


class WStream:
    def __init__(self, K):
        self.K = K
        self.buf = [K.sb(f"wbuf{i}", [128, WEL], BF16) for i in range(2)]
        self.slot = [Slot(K, f"w{i}") for i in range(2)]
        self.free = [None, None]
        self.plan = []
        self.issued = 0
        self.used = 0
        self.evs = []

    def add(self, tag, parts, kc, nb):
        assert kc * nb <= WEL
        self.plan.append((tag, parts, kc, nb))

    def view(self, i):
        _, _, kc, nb = self.plan[i]
        return self.buf[i % 2][:, 0:kc * nb].rearrange("p (k n) -> p k n", n=nb)

    def _issue(self, i):
        s = i % 2
        tag, parts, kc, nb = self.plan[i]
        self.K.pool.wait(self.free[s])
        v = self.view(i)
        ev = None
        for (lo, hi, src) in parts:
            ev = self.slot[s].dma(self.K.pool, v[:, :, lo:hi], src)
        self.evs.append(ev)

    def acquire(self, tag):
        i = self.used
        assert self.plan[i][0] == tag, (self.plan[i][0], tag)
        while self.issued < len(self.plan) and self.issued <= i + 1:
            self._issue(self.issued)
            self.issued += 1
        return self.view(i), self.evs[i]

    def release(self, ev):
        self.free[self.used % 2] = ev
        self.used += 1


def wsrc(w, c0, c1):
    return w.rearrange("(kc p) n -> p kc n", p=128)[:, :, c0:c1]


def alibi_slopes(n):
    return [2.0 ** (-8.0 * (i + 1) / n) for i in range(n)]


def build(stop=None):
    K = Kern()
    nc = K.nc
    pe, act, dve, pool, sp = K.pe, K.act, K.dve, K.pool, K.sp
    ps = K.ps

    def din(name, shape, dt=F32):
        return nc.dram_tensor(name, list(shape), dt, kind="ExternalInput").ap()

    x = din("x", [T, D])
    pT = din("pT", [2, 256, T])
    a_w_qkv = din("a_w_qkv", [D, 2560])
    a_w_o = din("a_w_o", [D, D])
    b_w_qkv = din("b_w_qkv", [D, 6144])
    b_w_o = din("b_w_o", [D, D])
    w_ffn_in = din("w_ffn_in", [2, D, 2 * DFF])
    w_ffn_out = din("w_ffn_out", [2, DFF, D])
    ple_w_proj = din("ple_w_proj", [2, 256, D])
    ple_w_gate = din("ple_w_gate", [2, D, D])
    gcols_d = din("gcols", [128, 6 * 16])
    postg_d = din("postg", [2, D])
    vecs_d = din("vecs", [1184])
    ident_d = din("ident", [128, 128])
    distA_d = din("distA", [128, 384], U16)
    distB_d = din("distB", [128, GW], U16)
    out = nc.dram_tensor("out", [T, D], F32, kind="ExternalOutput").ap()
    hres = nc.dram_tensor("hres", [T, D], F32).ap()
    obuf = nc.dram_tensor("obuf", [T, D], BF16).ap()

    R = K.sb("R", [128, REL], BF16)
    rtb = K.sb("rtb", [128, 2, 2048], F32)
    junk = K.sb("junk", [128, 1024], BF16)
    pcs = K.sb("pcs", [128, 4, 512], F32)
    scs = K.sb("scs", [128, 2, 512], F32)
    qbs = K.sb("qbs", [128, 2, 512], BF16)
    identF = K.sb("identF", [128, 128], F32)
    identB = K.sb("identB", [128, 128], BF16)
    gcols = K.sb("gcols_sb", [128, 96], F32)
    vecs = K.sb("vecs_sb", [128, 1184], F32)
    distA = K.sb("distA_sb", [128, 384], U16)
    small = K.sb("small", [128, 256], F32)
    epsc = K.sb("epsc", [128, 8], F32)
    W = WStream(K)

    def Rv(lo, n, inner):
        return R[:, lo:lo + n].rearrange("p (c t) -> p c t", t=inner)

    hT = Rv(0, 32768, 2048)
    QTg = Rv(32768, 8192, 2048)
    KTd = Rv(40960, 8192, 2048)
    VA = R[:, 49152:49152 + 4160].rearrange("p (t h d) -> p t h d", h=4, d=65)
    QTh = Rv(32768, 4096, 2048)
    KTh = Rv(36864, 4096, 2048)
    VB = R[:, 40960:40960 + 4112].rearrange("p (t d) -> p t d", d=257)
    GT = R[:, 53312:53312 + GW]
    GTu = GT.bitcast(U16)
    EP = R[:, 57280:57280 + 2048].rearrange("p (s n) -> p s n", n=512)
    actT = Rv(0, 45056, 1024)
    hTh = Rv(45056, 16384, 1024)
    pTs = Rv(32768, 4096, 2048)
    wpj = Rv(36864, 4096, 2048)
    wpj2 = Rv(40960, 4096, 2048)
    og = rtb[:, :, :].rearrange("p a b -> p (a b)").bitcast(BF16)[:, 0:8192].rearrange("p (t c) -> p t c", c=512)

    SS = 0
    RS = 16
    S8 = 32
    R8 = 48
    ESK = 64
    DEN = 96
    LAM = 104
    RP = 112
    SP4 = 128
    SSB = 192
    V_AQ, V_AK, V_SINK, V_BQ, V_BK, V_LAM, V_SUB = 0, 64, 128, 160, 288, 416, 928

    for l in range(2):
        if l == 0:
            W.add("a_kv", [(0, 256, wsrc(a_w_qkv, 2048, 2304)), (256, 512, wsrc(a_w_qkv, 2304, 2560))], 16, 512)
            for g in range(4):
                W.add(f"a_q{g}", [(0, 512, wsrc(a_w_qkv, g * 512, (g + 1) * 512))], 16, 512)
            wo = a_w_o
        else:
            for h in range(8):
                W.add(f"b_h{h}", [(0, 256, wsrc(b_w_qkv, h * 256, (h + 1) * 256)),
                                  (256, 512, wsrc(b_w_qkv, 2048 + h * 256, 2048 + (h + 1) * 256)),
                                  (512, 768, wsrc(b_w_qkv, 4096 + h * 256, 4096 + (h + 1) * 256))], 16, 768)
            wo = b_w_o
        for b in range(4):
            W.add(f"wo{l}_{b}", [(0, 512, wsrc(wo, b * 512, (b + 1) * 512))], 16, 512)
        for hf in range(2):
            for blk in range(22):
                W.add(f"fi{l}_{hf}_{blk}", [(0, 256, wsrc(w_ffn_in[l], blk * 256, (blk + 1) * 256)),
                                            (256, 512, wsrc(w_ffn_in[l], DFF + blk * 256, DFF + (blk + 1) * 256))], 16, 512)
            for b in range(8):
                W.add(f"fo{l}_{hf}_{b}", [(0, 256, wsrc(w_ffn_out[l], b * 256, (b + 1) * 256))], FC, 256)
        for b in range(4):
            W.add(f"wg{l}_{b}", [(0, 512, wsrc(ple_w_gate[l], b * 512, (b + 1) * 512))], 16, 512)

    cs = Slot(K, "const")
    cev = cs.dma(sp, identF[:], ident_d)
    cev = cs.dma(sp, gcols[:], gcols_d)
    cev = cs.dma(sp, vecs[:], vecs_d.partition_broadcast(128))
    cev = cs.dma(sp, distA[:], distA_d)
    cs2 = Slot(K, "const2")
    cev2 = cs2.dma(pool, identB[:], ident_d)
    dve.wait(cev)
    e = dve.mark(nc.vector.memset(small[:], 0.0))
    dve.wait(e)
    e = dve.mark(nc.vector.memset(epsc[:], EPS))
    act.wait(cev, e)
    esk_ev = act.mark(nc.scalar.activation(out=small[:, ESK:ESK + 32], in_=vecs[:, V_SINK:V_SINK + 32], func=AF.Exp))

    res_ev = [[None] * 8 for _ in range(NT)]
    st = {"rt_i": 0, "pc_i": 0, "sc_i": 0, "ev_i": 0}
    rt_slot = [Slot(K, "rt0"), Slot(K, "rt1")]
    rt_free = [None, None]
    junk_free = [None]
    pc_slot_in = [Slot(K, f"pci{i}") for i in range(4)]
    pc_slot_out = [Slot(K, f"pco{i}") for i in range(4)]
    pc_free = [None] * 4
    sc_free = [None, None]
    qb_free = [None, None]

    def res_src(l):
        return x if l == 0 else hres

    def evac_engine():
        st["ev_i"] += 1
        return act if st["ev_i"] % 2 == 0 else dve

    def rstd_op(dst, src_ap, n_inv, waits):
        act.wait(waits)
        e = act.mark(nc.scalar.activation(out=dst, in_=src_ap, func=AF.Ln, scale=n_inv, bias=epsc[:, 0:1]))
        act.wait(e)
        return act.mark(nc.scalar.activation(out=dst, in_=dst, func=AF.Exp, scale=-0.5))

    def copy_on(q, out_ap, in_ap):
        if q is act:
            return nc.scalar.copy(out=out_ap, in_=in_ap)
        return nc.vector.tensor_copy(out=out_ap, in_=in_ap)

    class Pro:
        def __init__(self, src, tiles, gci, dstT, tok0, dst_free_ev, bf16_src=False, src_wait=None, emit_all=True, dist=3):
            self.src, self.tiles, self.gci, self.dstT, self.tok0 = src, list(tiles), gci, dstT, tok0
            self.dst_free_ev, self.bf16_src, self.src_wait = dst_free_ev, bf16_src, src_wait
            self.ready = {}
            self.todo = list(tiles)
            self.inflight = []
            self.ld = {}
            self.k = 0
            self.dist = dist
            if emit_all:
                self.emit_rest()

        def _force_one(self):
            if self.inflight:
                tt, _ = self.inflight.pop(0)
                self.emit_pe(tt)
            elif self.todo:
                tt = self.todo.pop(0)
                self.emit_load(tt)
                self.emit_pe(tt)

        def emit_rest(self):
            while self.todo or self.inflight:
                self._force_one()

        def emit_next(self, n=1):
            for _ in range(n):
                self._force_one()

        def need(self, tt):
            while tt not in self.ready:
                self._force_one()

        def step(self, max_tile=None):
            self.k += 1
            while self.inflight and self.k - self.inflight[0][1] >= self.dist:
                tt, _ = self.inflight.pop(0)
                self.emit_pe(tt)
            while self.todo and len(self.inflight) < 2 and (max_tile is None or self.todo[0] <= max_tile):
                tt = self.todo.pop(0)
                self.emit_load(tt)
                self.inflight.append((tt, self.k))

        def rdy(self, tts):
            return [self.ready[t] for t in tts]

        def emit_load(self, tt):
            src = self.src
            s = st["rt_i"] % 2
            st["rt_i"] += 1
            sp.wait(rt_free[s], res_ev[tt] if self.src_wait is None else self.src_wait)
            if self.bf16_src:
                rt_bf = rtb[:, s, :].bitcast(BF16)[:, 0:2048]
                ev_ld = rt_slot[s].dma(sp, rt_bf, src[tt * 128:(tt + 1) * 128, :])
                self.ld[tt] = (s, ev_ld, rt_bf)
            else:
                rt = rtb[:, s, :]
                ev_ld = rt_slot[s].dma(sp, rt, src[tt * 128:(tt + 1) * 128, :])
                col = tt
                act.wait(ev_ld, junk_free[0])
                e1 = act.mark(nc.scalar.activation(out=junk[:], in_=rt[:, 0:1024], func=AF.Square,
                                                   accum_out=small[:, SS + col:SS + col + 1]))
                act.wait(e1)
                e1 = act.mark(nc.scalar.activation(out=junk[:], in_=rt[:, 1024:2048], func=AF.Square,
                                                   accum_out=small[:, SSB + col:SSB + col + 1]))
                junk_free[0] = e1
                dve.wait(e1)
                e2 = dve.mark(nc.vector.tensor_tensor(out=small[:, SS + col:SS + col + 1], in0=small[:, SS + col:SS + col + 1],
                                                      in1=small[:, SSB + col:SSB + col + 1], op=ALU.add))
                e3 = rstd_op(small[:, RS + col:RS + col + 1], small[:, SS + col:SS + col + 1], 1.0 / D, e2)
                dve.wait(e3)
                e4 = dve.mark(nc.vector.tensor_scalar(out=rt[:, 0:1024], in0=rt[:, 0:1024], scalar1=small[:, RS + col:RS + col + 1],
                                                      scalar2=None, op0=ALU.mult))
                act.wait(e3)
                e4b = act.mark(nc.scalar.activation(out=rt[:, 1024:2048], in_=rt[:, 1024:2048], func=AF.Copy,
                                                    scale=small[:, RS + col:RS + col + 1]))
                self.ld[tt] = (s, [e4, e4b], rt)

        def emit_pe(self, tt):
            gci, dstT, tok0, dst_free_ev = self.gci, self.dstT, self.tok0, self.dst_free_ev
            s, ready, rt = self.ld.pop(tt)
            t0 = tt * 128 - tok0
            evs = []
            if self.bf16_src:
                rt_bf = rt
                for cg in range(2):
                    bk = K.next_bank()
                    pe.wait(K.bank_free[bk], ready, cev2)
                    pb = K.bank_bf(bk)
                    for j in range(8):
                        c = cg * 8 + j
                        mm = nc.tensor.transpose(out=pb[:, j * 128:(j + 1) * 128], in_=rt_bf[:, c * 128:(c + 1) * 128],
                                                 identity=identB[:])
                    pev = pe.mark(mm)
                    q = evac_engine()
                    q.wait(pev, dst_free_ev)
                    e5 = q.mark(copy_on(q, dstT[:, cg * 8:(cg + 1) * 8, t0:t0 + 128],
                                        pb.rearrange("p (c t) -> p c t", t=128)))
                    K.bank_free[bk] = e5
                    evs.append(e5)
                rt_free[s] = pev
            else:
                for cg in range(4):
                    bk = K.next_bank()
                    pe.wait(K.bank_free[bk], ready, cev)
                    for j in range(4):
                        c = cg * 4 + j
                        mm = nc.tensor.transpose(out=ps[:, bk, j * 128:(j + 1) * 128], in_=rt[:, c * 128:(c + 1) * 128],
                                                 identity=identF[:])
                    pev = pe.mark(mm)
                    dve.wait(pev, dst_free_ev)
                    gb = gcols[:, gci * 16 + cg * 4:gci * 16 + cg * 4 + 4].unsqueeze(2).to_broadcast([128, 4, 128])
                    e5 = dve.mark(nc.vector.tensor_tensor(out=dstT[:, cg * 4:(cg + 1) * 4, t0:t0 + 128],
                                                          in0=ps[:, bk, :].rearrange("p (c t) -> p c t", t=128),
                                                          in1=gb, op=ALU.mult))
                    K.bank_free[bk] = e5
                    evs.append(e5)
                rt_free[s] = pev
            self.ready[tt] = evs

    class ResPieces:
        def __init__(self, l_src, dst, ncol, order, ahead=2):
            self.l_src, self.dst, self.ncol, self.order, self.ahead = l_src, dst, ncol, order, ahead
            self.loaded = []

        def _load(self, i):
            tt, c0 = self.order[i]
            s = st["pc_i"] % 4
            st["pc_i"] += 1
            pk = [c0 // 256 + k for k in range(self.ncol // 256)]
            sp.wait(pc_free[s], [res_ev[tt][k] for k in pk])
            ev_in = pc_slot_in[s].dma(sp, pcs[:, s, 0:self.ncol], self.l_src[tt * 128:(tt + 1) * 128, c0:c0 + self.ncol])
            self.loaded.append((s, ev_in))

        def finish(self, i, psum_ap, wait_evs):
            while len(self.loaded) < min(len(self.order), i + 1 + self.ahead):
                self._load(len(self.loaded))
            tt, c0 = self.order[i]
            s, ev_in = self.loaded[i]
            pc = pcs[:, s, 0:self.ncol]
            pk = [c0 // 256 + k for k in range(self.ncol // 256)]
            dve.wait(ev_in, wait_evs)
            e = dve.mark(nc.vector.tensor_tensor(out=pc, in0=psum_ap, in1=pc, op=ALU.add))
            sp.wait(e)
            ev_out = pc_slot_out[s].dma(sp, self.dst[tt * 128:(tt + 1) * 128, c0:c0 + self.ncol], pc)
            pc_free[s] = ev_out
            for k in pk:
                res_ev[tt][k] = ev_out
            return e

    def stop_here(tag):
        if stop != tag:
            return False
        allw = [ev for row in res_ev for ev in row]
        sp.wait(allw)
        ds = Slot(K, "dbg")
        e = ds.dma(sp, out, hres)
        sp.wait(e)
        return True

    LA = 3
    NEP = 8
    EP8 = R[:, 57280:57280 + 4096].rearrange("p (s n) -> p s n", n=512)
    ep_free = [None] * NEP
    epi = {"i": 0}
    proF = {}

    for l in range(2):
        src = res_src(l)
        if "proA_next" in st:
            proA = st.pop("proA_next")
            proA.dst_free_ev = st.get("hT_free")
            proA.step()
        else:
            proA = Pro(src, list(range(NT)), 0 + l, hT, 0, st.get("hT_free"), emit_all=False, dist=1)
            proA.step()
            proA.step()
        if l == 0:
            slopes = alibi_slopes(32)
            dve.wait(cev)
            ones_ev = dve.mark(nc.vector.memset(VA[:, :, :, 64:65], 1.0))
            gqk_ev = dve.mark(nc.vector.tensor_tensor(out=vecs[:, V_AK:V_AK + 64], in0=vecs[:, V_AK:V_AK + 64],
                                                      in1=vecs[:, V_AQ:V_AQ + 64], op=ALU.mult))
            wv, wev = W.acquire("a_kv")
            last_pe = None
            kv_ready = []
            pend = None
            for tt in range(NT):
                proA.need(tt)
                proA.step()
                bk = K.next_bank()
                pe.wait(K.bank_free[bk], wev, proA.ready[tt])
                for kc in range(DC):
                    mm = nc.tensor.matmul(ps[:, bk, :], lhsT=hT[:, kc, tt * 128:(tt + 1) * 128], rhs=wv[:, kc, :],
                                          start=(kc == 0), stop=(kc == DC - 1))
                pev = pe.mark(mm)
                last_pe = pev
                if pend is not None:
                    pend()
                s = st["sc_i"] % 2
                st["sc_i"] += 1
                sc = scs[:, s, 0:256]
                act.wait(pev, sc_free[s])
                e1 = act.mark(nc.scalar.activation(out=sc, in_=ps[:, bk, 0:256], func=AF.Square))
                act.wait(ones_ev)
                e1v = act.mark(nc.scalar.copy(out=VA[:, tt, :, 0:64], in_=ps[:, bk, 256:512].rearrange("p (h d) -> p h d", d=64)))
                kv_ready.append(e1v)
                dve.wait(e1)
                e2 = dve.mark(nc.vector.tensor_reduce(out=small[:, S8:S8 + 4], in_=sc.rearrange("p (h d) -> p h d", d=64),
                                                      axis=AX.X, op=ALU.add))
                e4 = rstd_op(small[:, R8:R8 + 4], small[:, S8:S8 + 4], 1.0 / 64, e2)
                dve.wait(e4)
                e5 = dve.mark(nc.vector.tensor_tensor(out=sc.rearrange("p (h d) -> p h d", d=64),
                                                      in0=ps[:, bk, 0:256].rearrange("p (h d) -> p h d", d=64),
                                                      in1=small[:, R8:R8 + 4].unsqueeze(2).to_broadcast([128, 4, 64]),
                                                      op=ALU.mult))
                K.bank_free[bk] = [e5, e1v]
                dve.wait(e5, qb_free[s], gqk_ev)
                qb = qbs[:, s, :]
                e6 = dve.mark(nc.vector.tensor_tensor(
                    out=qb.rearrange("p (h r d) -> p h r d", r=2, d=64),
                    in0=sc.rearrange("p (h d) -> p h d", d=64).unsqueeze(2).to_broadcast([128, 4, 2, 64]),
                    in1=vecs[:, V_AK:V_AK + 64].unsqueeze(1).unsqueeze(1).to_broadcast([128, 4, 2, 64]),
                    op=ALU.mult))
                sc_free[s] = e6

                def fin(tt=tt, s=s, qb=qb, e6=e6):
                    bk2 = K.next_bank()
                    pe.wait(K.bank_free[bk2], e6, cev2)
                    pb = K.bank_bf(bk2)
                    for j in range(4):
                        mm2 = nc.tensor.transpose(out=pb[:, j * 128:(j + 1) * 128], in_=qb[:, j * 128:(j + 1) * 128], identity=identB[:])
                    pev2 = pe.mark(mm2)
                    qb_free[s] = pev2
                    q = evac_engine()
                    q.wait(pev2)
                    e7 = q.mark(copy_on(q, KTd[:, :, tt * 128:(tt + 1) * 128], pb[:, 0:512].rearrange("p (c t) -> p c t", t=128)))
                    K.bank_free[bk2] = e7
                    kv_ready.append(e7)
                pend = fin
            pend()
            proA.emit_rest()
            W.release(last_pe)
            og_free = None
            g_free = None
            pairc = {"s": 0, "o": 0, "g": 0}
            g_free2 = [None, None]
            for g in range(4):
                wv, wev = W.acquire(f"a_q{g}")
                last_pe = None
                q_ready = []
                pend = None
                for tt in range(NT):
                    bk = K.next_bank()
                    pe.wait(K.bank_free[bk], wev, proA.ready[tt])
                    for kc in range(DC):
                        mm = nc.tensor.matmul(ps[:, bk, :], lhsT=hT[:, kc, tt * 128:(tt + 1) * 128], rhs=wv[:, kc, :],
                                              start=(kc == 0), stop=(kc == DC - 1))
                    pev = pe.mark(mm)
                    last_pe = pev
                    if pend is not None:
                        pend()
                    s = st["sc_i"] % 2
                    st["sc_i"] += 1
                    sc = scs[:, s, :]
                    act.wait(pev, sc_free[s])
                    e1 = act.mark(nc.scalar.activation(out=sc, in_=ps[:, bk, :], func=AF.Square))
                    dve.wait(e1)
                    e2 = dve.mark(nc.vector.tensor_reduce(out=small[:, S8:S8 + 8], in_=sc.rearrange("p (h d) -> p h d", d=64),
                                                          axis=AX.X, op=ALU.add))
                    e4 = rstd_op(small[:, R8:R8 + 8], small[:, S8:S8 + 8], 1.0 / 64, e2)
                    dve.wait(e4, qb_free[s])
                    qb = qbs[:, s, :]
                    e6 = dve.mark(nc.vector.tensor_tensor(out=qb.rearrange("p (h d) -> p h d", d=64),
                                                          in0=ps[:, bk, :].rearrange("p (h d) -> p h d", d=64),
                                                          in1=small[:, R8:R8 + 8].unsqueeze(2).to_broadcast([128, 8, 64]),
                                                          op=ALU.mult))
                    K.bank_free[bk] = e6
                    sc_free[s] = e2

                    def fin(tt=tt, s=s, qb=qb, e6=e6):
                        bk2 = K.next_bank()
                        pe.wait(K.bank_free[bk2], e6, cev2)
                        pb = K.bank_bf(bk2)
                        for j in range(4):
                            mm2 = nc.tensor.transpose(out=pb[:, j * 128:(j + 1) * 128], in_=qb[:, j * 128:(j + 1) * 128], identity=identB[:])
                        pev2 = pe.mark(mm2)
                        qb_free[s] = pev2
                        q = evac_engine()
                        q.wait(pev2, st.get("qt_free"))
                        e7 = q.mark(copy_on(q, QTg[:, :, tt * 128:(tt + 1) * 128], pb[:, 0:512].rearrange("p (c t) -> p c t", t=128)))
                        K.bank_free[bk2] = e7
                        q_ready.append(e7)
                    pend = fin
                pend()
                W.release(last_pe)
                for hp in range(4):
                    hA = g * 8 + 2 * hp
                    c = hp
                    gb_ = pairc["g"] % 2
                    pairc["g"] += 1
                    go = gb_ * 768
                    act.wait(cev, g_free2[gb_])
                    g_ev = act.mark(nc.scalar.activation(out=GT[:, go:go + 384], in_=distA[:], func=AF.Exp, scale=-slopes[hA]))
                    g_ev = act.mark(nc.scalar.activation(out=GT[:, go + 384:go + 768], in_=distA[:], func=AF.Exp, scale=-slopes[hA + 1]))
                    GT2 = GT[:, go:go + 768].rearrange("p (r n) -> p r n", n=384)
                    pq = []
                    o_last = None
                    LAP = 3
                    for i in range(NT + LAP):
                        if i < NT:
                            n = i
                            j0, j1 = max(n - 1, 0), min(n + 1, NT - 1)
                            a, b_ = (j0 - n + 1) * 128, (j1 - n + 2) * 128
                            sb = 2 * (pairc["s"] % 2)
                            pairc["s"] += 1
                            pe.wait(K.bank_free[sb], K.bank_free[sb + 1], q_ready, kv_ready)
                            for j in range(j0, j1 + 1):
                                cc = (j - n + 1) * 128
                                for r in range(2):
                                    mm = nc.tensor.matmul(ps[:, sb + r, cc:cc + 128],
                                                          lhsT=KTd[r * 64:(r + 1) * 64, g, j * 128:(j + 1) * 128],
                                                          rhs=QTg[r * 64:(r + 1) * 64, c, n * 128:(n + 1) * 128], start=True, stop=True)
                            pev = pe.mark(mm)
                            es = 2 * (epi["i"] % 4)
                            epi["i"] += 1
                            act.wait(pev, ep_free[es], ep_free[es + 1])
                            e1 = act.mark(nc.scalar.activation(out=EP8[:, es:es + 2, a:b_], in_=ps[:, sb:sb + 2, a:b_], func=AF.Exp, scale=0.125))
                            K.bank_free[sb] = e1
                            K.bank_free[sb + 1] = e1
                            dve.wait(e1, g_ev)
                            e2 = dve.mark(nc.vector.tensor_tensor(out=EP8[:, es:es + 2, a:b_], in0=EP8[:, es:es + 2, a:b_], in1=GT2[:, :, a:b_], op=ALU.mult))
                            pq.append((n, es, j0, j1, e2))
                        if i >= LAP:
                            pn, pes, pj0, pj1, pev_p = pq[i - LAP]
                            qq = pn % 4
                            if qq == 0:
                                ob = 4 + 2 * (pairc["o"] % 2)
                                pairc["o"] += 1
                                pe.wait(K.bank_free[ob], K.bank_free[ob + 1])
                            pe.wait(pev_p, kv_ready)
                            for r in range(2):
                                for j in range(pj0, pj1 + 1):
                                    cc = (j - pn + 1) * 128
                                    mm = nc.tensor.matmul(ps[:, ob + r, qq * 65:qq * 65 + 65], lhsT=EP8[:, pes + r, cc:cc + 128],
                                                          rhs=VA[:, j, g, :], start=(j == pj0), stop=(j == pj1))
                            pev2 = pe.mark(mm)
                            ep_free[pes] = pev2
                            ep_free[pes + 1] = pev2
                            if qq == 3:
                                for r in range(2):
                                    h = hA + r
                                    hl = 2 * hp + r
                                    ov = ps[:, ob + r, 0:260].rearrange("p (q d) -> p q d", d=65)
                                    dve.wait(pev2, esk_ev)
                                    d1 = dve.mark(nc.vector.tensor_scalar(out=small[:, DEN + 4 * r:DEN + 4 * r + 4], in0=ov[:, :, 64],
                                                                          scalar1=small[:, ESK + h:ESK + h + 1], scalar2=None, op0=ALU.add))
                                    dve.wait(d1)
                                    d2 = dve.mark(nc.vector.reciprocal(out=small[:, DEN + 4 * r:DEN + 4 * r + 4], in_=small[:, DEN + 4 * r:DEN + 4 * r + 4]))
                                    dve.wait(d2, og_free if (hl == 0 and pn == 3) else None)
                                    d3 = dve.mark(nc.vector.tensor_tensor(out=og[:, pn - 3:pn + 1, hl * 64:(hl + 1) * 64], in0=ov[:, :, 0:64],
                                                                          in1=small[:, DEN + 4 * r:DEN + 4 * r + 4].unsqueeze(2).to_broadcast([128, 4, 64]),
                                                                          op=ALU.mult))
                                    K.bank_free[ob + r] = d3
                                    o_last = d3
                    g_free2[gb_] = o_last
                    g_free = o_last
                st["qt_free"] = o_last
                sp.wait(o_last)
                ogs = Slot(K, f"og{l}_{g}")
                og_free = ogs.dma(sp, obuf.rearrange("(n p) c -> p n c", p=128)[:, :, g * 512:(g + 1) * 512], og)
                st["o_written"] = og_free
            rt_free[0] = og_free
            rt_free[1] = og_free
        else:
            slopes = alibi_slopes(8)
            scale1 = 128.0 ** -0.5
            lambda_init = 0.8 - 0.6 * math.exp(-0.3 * 1)
            QKh = Rv(32768, 8192, 2048)
            qkg = R[:, 45312:46336].bitcast(F32)
            ogh = rtb[:, :, :].rearrange("p a b -> p (a b)").bitcast(BF16)[:, 0:4096].rearrange("p (t c) -> p t c", c=256)
            o1 = pcs[:, 0:2, :].rearrange("p a b -> p (a b)").rearrange("p (q d) -> p q d", d=256)
            afree = st.get("actT_free")
            lv = vecs[:, V_LAM:V_LAM + 512].rearrange("p (a b d) -> p a b d", b=2, d=128)
            dve.wait(cev, sc_free[0], sc_free[1])
            l1e = dve.mark(nc.vector.tensor_tensor(out=scs[:, 0, 0:256].rearrange("p (a d) -> p a d", d=128), in0=lv[:, :, 0, :],
                                                   in1=lv[:, :, 1, :], op=ALU.mult))
            dve.wait(l1e)
            l2e = dve.mark(nc.vector.tensor_reduce(out=small[:, LAM:LAM + 2], in_=scs[:, 0, 0:256].rearrange("p (a d) -> p a d", d=128),
                                                   axis=AX.X, op=ALU.add))
            act.wait(l2e)
            l3e = act.mark(nc.scalar.activation(out=small[:, LAM:LAM + 2], in_=small[:, LAM:LAM + 2], func=AF.Exp))
            dve.wait(l3e)
            l4e = dve.mark(nc.vector.tensor_tensor(out=small[:, LAM + 2:LAM + 3], in0=small[:, LAM:LAM + 1], in1=small[:, LAM + 1:LAM + 2],
                                                   op=ALU.subtract))
            dve.wait(l4e)
            lam_ev = dve.mark(nc.vector.tensor_scalar(out=small[:, LAM + 3:LAM + 4], in0=small[:, LAM + 2:LAM + 3], scalar1=-1.0,
                                                      scalar2=-lambda_init, op0=ALU.mult, op1=ALU.add))
            dve.wait(afree)
            for i in range(4):
                off = V_BQ if i < 2 else V_BK
                gq_ev = dve.mark(nc.vector.tensor_copy(out=qkg[:, i * 128:(i + 1) * 128], in_=vecs[:, off:off + 128]))
            ones_ev = dve.mark(nc.vector.memset(VB[:, :, 256:257], 1.0))
            og_free = None
            g_free = None
            qk_free = None
            v_free = None
            sb_i = 0
            o_last = None
            GTbufs = [R[:, 53312:53312 + GW], R[:, 46336:46336 + GW]]
            g_frees = [None, None]
            gl = {}

            def g_dma(hh):
                sp.wait(g_frees[hh % 2], afree)
                gl[hh] = Slot(K, f"g{hh}").dma(sp, GTbufs[hh % 2].bitcast(U16), distB_d)
            for h in range(8):
                GTh = GTbufs[h % 2]
                GThu = GTh.bitcast(U16)
                if h == 0:
                    g_dma(0)
                gl_ev = gl[h]
                g_ev = None
                for i in range(4):
                    act.wait(gl_ev)
                    g_ev = act.mark(nc.scalar.activation(out=GTh[:, i * 992:(i + 1) * 992], in_=GThu[:, i * 992:(i + 1) * 992],
                                                         func=AF.Exp, scale=-slopes[h]))
                wv, wev = W.acquire(f"b_h{h}")
                last_pe = None
                qk_ready = []
                pend = None
                for tt in range(NT):
                    if h == 0:
                        proA.need(tt)
                        proA.step()
                    bk = K.next_bank()
                    pe.wait(K.bank_free[bk], wev, proA.ready[tt])
                    for kc in range(DC):
                        mm = nc.tensor.matmul(ps[:, bk, :], lhsT=hT[:, kc, tt * 128:(tt + 1) * 128], rhs=wv[:, kc, 0:512],
                                              start=(kc == 0), stop=(kc == DC - 1))
                    bkv = K.next_bank()
                    pe.wait(K.bank_free[bkv])
                    for kc in range(DC):
                        mm = nc.tensor.matmul(ps[:, bkv, 0:256], lhsT=hT[:, kc, tt * 128:(tt + 1) * 128], rhs=wv[:, kc, 512:768],
                                              start=(kc == 0), stop=(kc == DC - 1))
                    pev = pe.mark(mm)
                    last_pe = pev
                    if pend is not None:
                        pend()
                    s = st["sc_i"] % 2
                    st["sc_i"] += 1
                    sc = scs[:, s, :]
                    act.wait(pev, sc_free[s], l2e)
                    e1 = act.mark(nc.scalar.activation(out=sc, in_=ps[:, bk, :], func=AF.Square))
                    act.wait(v_free, ones_ev)
                    e1v = act.mark(nc.scalar.copy(out=VB[:, tt, 0:256], in_=ps[:, bkv, 0:256]))
                    K.bank_free[bkv] = e1v
                    qk_ready.append(e1v)
                    dve.wait(e1)
                    e2 = dve.mark(nc.vector.tensor_reduce(out=small[:, S8:S8 + 4], in_=sc.rearrange("p (h d) -> p h d", d=128),
                                                          axis=AX.X, op=ALU.add))
                    e4 = rstd_op(small[:, R8:R8 + 4], small[:, S8:S8 + 4], 1.0 / 128, e2)
                    dve.wait(e4)
                    e5 = dve.mark(nc.vector.tensor_tensor(out=sc.rearrange("p (h d) -> p h d", d=128),
                                                          in0=ps[:, bk, :].rearrange("p (h d) -> p h d", d=128),
                                                          in1=small[:, R8:R8 + 4].unsqueeze(2).to_broadcast([128, 4, 128]),
                                                          op=ALU.mult))
                    K.bank_free[bk] = e5
                    dve.wait(e5, qb_free[s], gq_ev)
                    qb = qbs[:, s, :]
                    e6 = dve.mark(nc.vector.tensor_tensor(out=qb, in0=sc, in1=qkg, op=ALU.mult))
                    sc_free[s] = e6

                    def fin(tt=tt, s=s, qb=qb, e6=e6):
                        bk2 = K.next_bank()
                        pe.wait(K.bank_free[bk2], e6, cev2)
                        pb = K.bank_bf(bk2)
                        for j in range(4):
                            mm2 = nc.tensor.transpose(out=pb[:, j * 128:(j + 1) * 128], in_=qb[:, j * 128:(j + 1) * 128], identity=identB[:])
                        pev2 = pe.mark(mm2)
                        qb_free[s] = pev2
                        q = evac_engine()
                        q.wait(pev2, qk_free)
                        e7 = q.mark(copy_on(q, QKh[:, :, tt * 128:(tt + 1) * 128], pb[:, 0:512].rearrange("p (c t) -> p c t", t=128)))
                        K.bank_free[bk2] = e7
                        qk_ready.append(e7)
                    pend = fin
                pend()
                if h == 0:
                    proA.emit_rest()
                W.release(last_pe)
                if h + 1 < 8:
                    g_dma(h + 1)
                steps = [(qb_, c, j) for qb_ in range(4) for c in range(2) for j in range(NT)]
                pq = []
                last_s_pe = None
                pev2 = None
                LA1 = 7
                for i in range(len(steps) + LA1):
                    if i < len(steps):
                        qb_, c, j = steps[i]
                        bk = 4 + (sb_i % 4)
                        sb_i += 1
                        pe.wait(K.bank_free[bk], qk_ready)
                        mm = nc.tensor.matmul(ps[:, bk, :], lhsT=QKh[:, 2 + c, j * 128:(j + 1) * 128],
                                              rhs=QKh[:, c, qb_ * 512:(qb_ + 1) * 512], start=True, stop=True)
                        pev = pe.mark(mm)
                        last_s_pe = pev
                        es = epi["i"] % NEP
                        epi["i"] += 1
                        act.wait(pev, ep_free[es])
                        e1 = act.mark(nc.scalar.activation(out=EP8[:, es, :], in_=ps[:, bk, :], func=AF.Exp, scale=scale1))
                        K.bank_free[bk] = e1
                        off = qb_ * 512 - j * 128 + GOFF
                        dve.wait(e1, g_ev)
                        e2 = dve.mark(nc.vector.tensor_tensor(out=EP8[:, es, :], in0=EP8[:, es, :], in1=GTh[:, off:off + 512], op=ALU.mult))
                        pq.append((qb_, c, j, es, e2))
                    if i >= LA1:
                        qb_, c, j, es, e2 = pq[i - LA1]
                        pe.wait(e2, qk_ready)
                        for qq in range(4):
                            if j == 0:
                                pe.wait(K.bank_free[qq])
                            mm = nc.tensor.matmul(ps[:, qq, 0:257], lhsT=EP8[:, es, qq * 128:(qq + 1) * 128], rhs=VB[:, j, :],
                                                  start=(j == 0), stop=(j == NT - 1))
                        pev2 = pe.mark(mm)
                        ep_free[es] = pev2
                        if j == NT - 1:
                            if c == 0:
                                dve.wait(pev2, [pc_free[k_] for k_ in range(4)], o_last)
                                d1 = dve.mark(nc.vector.reciprocal(out=small[:, DEN:DEN + 4], in_=ps[:, 0:4, 256]))
                                dve.wait(d1)
                                for qq in range(4):
                                    d2 = dve.mark(nc.vector.tensor_scalar(out=o1[:, qq, :], in0=ps[:, qq, 0:256],
                                                                          scalar1=small[:, DEN + qq:DEN + qq + 1], scalar2=None, op0=ALU.mult))
                                    K.bank_free[qq] = d2
                            else:
                                dve.wait(pev2)
                                d1 = dve.mark(nc.vector.reciprocal(out=small[:, DEN + 4:DEN + 8], in_=ps[:, 0:4, 256]))
                                dve.wait(d1, lam_ev)
                                d2 = dve.mark(nc.vector.tensor_scalar(out=small[:, DEN + 4:DEN + 8], in0=small[:, DEN + 4:DEN + 8],
                                                                      scalar1=small[:, LAM + 3:LAM + 4], scalar2=None, op0=ALU.mult))
                                dve.wait(d2)
                                for qq in range(4):
                                    d4 = dve.mark(nc.vector.scalar_tensor_tensor(out=o1[:, qq, :], in0=ps[:, qq, 0:256],
                                                                                 scalar=small[:, DEN + 4 + qq:DEN + 5 + qq], in1=o1[:, qq, :],
                                                                                 op0=ALU.mult, op1=ALU.add))
                                    K.bank_free[qq] = d4
                                a_ev = None
                                for qq in range(4):
                                    act.wait(d4, junk_free[0])
                                    a_ev = act.mark(nc.scalar.activation(out=junk[:, 0:256], in_=o1[:, qq, :], func=AF.Square,
                                                                         accum_out=small[:, S8 + 8 + qq:S8 + 9 + qq]))
                                    junk_free[0] = a_ev
                                act.wait(a_ev)
                                r_ev = act.mark(nc.scalar.activation(out=small[:, R8 + 8:R8 + 12], in_=small[:, S8 + 8:S8 + 12],
                                                                     func=AF.Ln, scale=1.0 / 256, bias=epsc[:, 0:1]))
                                act.wait(r_ev)
                                r_ev = act.mark(nc.scalar.activation(out=small[:, R8 + 8:R8 + 12], in_=small[:, R8 + 8:R8 + 12],
                                                                     func=AF.Exp, scale=-0.5))
                                dve.wait(r_ev)
                                d5 = dve.mark(nc.vector.tensor_tensor(out=o1, in0=o1,
                                                                      in1=small[:, R8 + 8:R8 + 12].unsqueeze(2).to_broadcast([128, 4, 256]),
                                                                      op=ALU.mult))
                                dve.wait(d5, og_free if qb_ == 0 else None, rt_free[0] if (h == 0 and qb_ == 0) else None,
                                         rt_free[1] if (h == 0 and qb_ == 0) else None)
                                d6 = dve.mark(nc.vector.scalar_tensor_tensor(
                                    out=ogh[:, qb_ * 4:(qb_ + 1) * 4, :], in0=o1, scalar=1.0 - lambda_init,
                                    in1=vecs[:, V_SUB:V_SUB + 256].unsqueeze(1).to_broadcast([128, 4, 256]),
                                    op0=ALU.mult, op1=ALU.mult))
                                o_last = d6
                g_frees[h % 2] = o_last
                qk_free = last_s_pe
                v_free = pev2
                sp.wait(o_last)
                ogs = Slot(K, f"og{l}_{h}")
                og_free = ogs.dma(sp, obuf.rearrange("(n p) c -> p n c", p=128)[:, :, h * 256:(h + 1) * 256], ogh)
                st["o_written"] = og_free
            rt_free[0] = og_free
            rt_free[1] = og_free
            for k_ in range(4):
                pc_free[k_] = o_last
        if stop == f"att{l}":
            break
        oT = hT
        proO = Pro(obuf, list(range(NT)), None, oT, 0, st.get("hT_free"), bf16_src=True, src_wait=st["o_written"], emit_all=False, dist=1)
        proO.step()
        proO.step()
        proF[0] = Pro(hres, list(range(0, 8)), 2 + l, hTh, 0, st.get("hTh_free"), emit_all=False, dist=4)
        rp = ResPieces(src, hres, 512, [(tt, b * 512) for b in range(4) for tt in range(NT)])
        for b in range(4):
            wv, wev = W.acquire(f"wo{l}_{b}")
            last_pe = None
            for tt in range(NT):
                if b == 0:
                    proO.need(tt)
                    proO.step()
                bk = K.next_bank()
                pe.wait(K.bank_free[bk], wev, proO.ready[tt])
                for kc in range(DC):
                    mm = nc.tensor.matmul(ps[:, bk, :], lhsT=oT[:, kc, tt * 128:(tt + 1) * 128], rhs=wv[:, kc, :],
                                          start=(kc == 0), stop=(kc == DC - 1))
                pev = pe.mark(mm)
                last_pe = pev
                K.bank_free[bk] = rp.finish(b * NT + tt, ps[:, bk, :], pev)
                if b == 3:
                    proF[0].step(max_tile=tt)
            W.release(last_pe)
        st["hT_free"] = last_pe
        if stop_here(f"wo{l}"):
            break

        for hf in range(2):
            tiles = list(range(hf * 8, hf * 8 + 8))
            tok0 = hf * 1024
            pf = proF[hf]
            pf.emit_rest()
            last_pe = None
            e2 = None
            for blk in range(22):
                wv, wev = W.acquire(f"fi{l}_{hf}_{blk}")
                for tb in range(2):
                    for m in range(2):
                        bg = K.next_bank()
                        pe.wait(K.bank_free[bg], wev, pf.rdy(tiles[tb * 4:(tb + 1) * 4]))
                        for kc in range(DC):
                            mm = nc.tensor.matmul(ps[:, bg, :], lhsT=wv[:, kc, m * 128:(m + 1) * 128],
                                                  rhs=hTh[:, kc, tb * 512:(tb + 1) * 512], start=(kc == 0), stop=(kc == DC - 1))
                        bu = K.next_bank()
                        pe.wait(K.bank_free[bu])
                        for kc in range(DC):
                            mm = nc.tensor.matmul(ps[:, bu, :], lhsT=wv[:, kc, 256 + m * 128:256 + (m + 1) * 128],
                                                  rhs=hTh[:, kc, tb * 512:(tb + 1) * 512], start=(kc == 0), stop=(kc == DC - 1))
                        pev = pe.mark(mm)
                        last_pe = pev
                        s = st["sc_i"] % 2
                        st["sc_i"] += 1
                        act.wait(pev, sc_free[s])
                        e1 = act.mark(nc.scalar.activation(out=scs[:, s, :], in_=ps[:, bg, :], func=AF.Silu))
                        K.bank_free[bg] = e1
                        dve.wait(e1, st.get("actT_free"))
                        e2 = dve.mark(nc.vector.tensor_tensor(out=actT[:, blk * 2 + m, tb * 512:(tb + 1) * 512], in0=ps[:, bu, :],
                                                              in1=scs[:, s, :], op=ALU.mult))
                        K.bank_free[bu] = e2
                        sc_free[s] = e2
                W.release(last_pe)
            st["hTh_free"] = last_pe
            act_ready = e2
            if hf == 1:
                hTa = hTh
                proGa = Pro(hres, list(range(0, 8)), 4 + l, hTa, 0, st["hTh_free"], emit_all=False, dist=3)
            if hf == 0:
                proF[1] = Pro(hres, list(range(8, 16)), 2 + l, hTh, 1024, st["hTh_free"], emit_all=False, dist=3)
            rp = ResPieces(hres, hres, 256, [(tt, b * 256) for b in range(8) for tt in tiles])
            for b in range(8):
                wv, wev = W.acquire(f"fo{l}_{hf}_{b}")
                for ti, tt in enumerate(tiles):
                    if hf == 0 and b >= 5:
                        proF[1].step()
                    if hf == 1:
                        proGa.step()
                    bk = K.next_bank()
                    pe.wait(K.bank_free[bk], wev, act_ready)
                    for kc in range(FC):
                        mm = nc.tensor.matmul(ps[:, bk, 0:256], lhsT=actT[:, kc, ti * 128:(ti + 1) * 128], rhs=wv[:, kc, :],
                                              start=(kc == 0), stop=(kc == FC - 1))
                    pev = pe.mark(mm)
                    last_pe = pev
                    K.bank_free[bk] = rp.finish(b * 8 + ti, ps[:, bk, 0:256], pev)
                W.release(last_pe)
            st["actT_free"] = last_pe
        if stop_here(f"ffn{l}"):
            break

        def stage(sidx, src_ap):
            sp.wait(rt_free[sidx])
            return rt_slot[sidx].dma(sp, rtb[:, sidx, :], src_ap)

        afree = st["actT_free"]
        pls = Slot(K, f"plew{l}")
        pool.wait(afree)
        pls.dma(pool, wpj[:, 0, :], ple_w_proj[l, 0:128, :])
        wpj_ev = pls.dma(pool, wpj[:, 1, :], ple_w_proj[l, 128:256, :])
        ev0 = stage(0, pT[l, 0:128, :])
        ev1 = stage(1, pT[l, 128:256, :])
        act.wait(ev0, afree)
        ea = act.mark(nc.scalar.copy(out=pTs[:, 0, :], in_=rtb[:, 0, :]))
        rt_free[0] = ea
        dve.wait(ev1, afree)
        eb = dve.mark(nc.vector.tensor_copy(out=pTs[:, 1, :], in_=rtb[:, 1, :]))
        rt_free[1] = eb
        pT_ready = [ea, eb]
        ev0 = stage(0, postg_d[l].partition_broadcast(128))
        wp_ready = []
        wpj_ready = [wpj_ev]
        for kc in range(2):
            ev1 = stage(1, ple_w_proj[l, kc * 128:(kc + 1) * 128, :])
            dve.wait(ev1, ev0, afree)
            eb = dve.mark(nc.vector.tensor_tensor(out=wpj2[:, kc, :], in0=rtb[:, 1, :], in1=rtb[:, 0, :], op=ALU.mult))
            rt_free[1] = eb
            wp_ready.append(eb)
        rt_free[0] = eb
        proGa.emit_rest()
        hTb = Rv(0, 16384, 1024)
        proGb = Pro(hres, list(range(8, 16)), 4 + l, hTb, 1024, afree, emit_all=False, dist=3)
        e1 = None
        for tt in range(NT):
            for b in range(4):
                bk = K.next_bank()
                pe.wait(K.bank_free[bk], pT_ready, wpj_ready)
                for kc in range(2):
                    mm = nc.tensor.matmul(ps[:, bk, :], lhsT=pTs[:, kc, tt * 128:(tt + 1) * 128], rhs=wpj[:, kc, b * 512:(b + 1) * 512],
                                          start=(kc == 0), stop=(kc == 1))
                pev = pe.mark(mm)
                act.wait(pev, junk_free[0])
                e1 = act.mark(nc.scalar.activation(out=junk[:, 0:512], in_=ps[:, bk, :], func=AF.Square,
                                                   accum_out=small[:, SP4 + tt * 4 + b:SP4 + tt * 4 + b + 1]))
                junk_free[0] = e1
                K.bank_free[bk] = e1
            if tt % 4 == 3 and tt >= 7:
                proGb.step()
        dve.wait(e1)
        e2 = dve.mark(nc.vector.tensor_reduce(out=small[:, RP:RP + 16], in_=small[:, SP4:SP4 + 64].rearrange("p (t b) -> p t b", b=4),
                                              axis=AX.X, op=ALU.add))
        rp_ev = rstd_op(small[:, RP:RP + 16], small[:, RP:RP + 16], 1.0 / D, e2)
        dst = out if l == 1 else hres
        rp = ResPieces(hres, dst, 512, [(tt, b * 512) for b in range(4) for tt in range(NT)])
        for b in range(4):
            wv, wev = W.acquire(f"wg{l}_{b}")
            last_pe = None
            for tt in range(NT):
                if b == 3 and tt == 12 and l == 0 and stop is None:
                    st["proA_next"] = Pro(hres, list(range(NT)), 1, hT, 0, None, emit_all=False, dist=1)
                    st["proA_next"].step()
                if b == 0:
                    if tt >= 8:
                        proGb.need(tt)
                    proGb.step()
                hsrc, tloc, rdy = (hTa, tt, proGa.ready[tt]) if tt < 8 else (hTb, tt - 8, proGb.ready[tt])
                bg = K.next_bank()
                pe.wait(K.bank_free[bg], wev, rdy)
                for kc in range(DC):
                    mm = nc.tensor.matmul(ps[:, bg, :], lhsT=hsrc[:, kc, tloc * 128:(tloc + 1) * 128], rhs=wv[:, kc, :],
                                          start=(kc == 0), stop=(kc == DC - 1))
                bp = K.next_bank()
                pe.wait(K.bank_free[bp], wp_ready)
                for kc in range(2):
                    mm = nc.tensor.matmul(ps[:, bp, :], lhsT=pTs[:, kc, tt * 128:(tt + 1) * 128], rhs=wpj2[:, kc, b * 512:(b + 1) * 512],
                                          start=(kc == 0), stop=(kc == 1))
                pev = pe.mark(mm)
                last_pe = pev
                s = st["sc_i"] % 2
                st["sc_i"] += 1
                act.wait(pev, sc_free[s])
                e1 = act.mark(nc.scalar.activation(out=scs[:, s, :], in_=ps[:, bg, :], func=AF.Sigmoid))
                K.bank_free[bg] = e1
                dve.wait(e1, rp_ev)
                e2 = dve.mark(nc.vector.scalar_tensor_tensor(out=scs[:, s, :], in0=ps[:, bp, :], scalar=small[:, RP + tt:RP + tt + 1],
                                                             in1=scs[:, s, :], op0=ALU.mult, op1=ALU.mult))
                K.bank_free[bp] = e2
                sc_free[s] = rp.finish(b * NT + tt, scs[:, s, :], e2)
            W.release(last_pe)
        st["hT_free"] = last_pe
        st["hTh_free"] = last_pe
        if stop_here(f"ple{l}"):
            break

    sp.wait([ev for row in res_ev for ev in row])
    K.es.close()
    return nc


def _host_inputs(inputs):
    f = np.float32
    x = np.asarray(inputs["x"], dtype=f)
    p = np.asarray(inputs["p"], dtype=f)
    common = {}
    common["a_w_qkv"] = np.ascontiguousarray(inputs["a_w_qkv"][0], dtype=f)
    common["a_w_o"] = np.ascontiguousarray(inputs["a_w_o"][0], dtype=f)
    common["b_w_qkv"] = np.ascontiguousarray(inputs["b_w_qkv"][0], dtype=f)
    common["b_w_o"] = np.ascontiguousarray(inputs["b_w_o"][0], dtype=f)
    common["w_ffn_in"] = np.ascontiguousarray(inputs["w_ffn_in"], dtype=f)
    common["w_ffn_out"] = np.ascontiguousarray(inputs["w_ffn_out"], dtype=f)
    common["ple_w_proj"] = np.ascontiguousarray(inputs["ple_w_proj"], dtype=f)
    common["ple_w_gate"] = np.ascontiguousarray(inputs["ple_w_gate"], dtype=f)
    gl = []
    for name in ("attn_norm", "ffn_norm", "ple_gate_norm"):
        for i in range(2):
            gl.append(np.asarray(inputs[name][i], dtype=f).reshape(16, 128).T)
    common["gcols"] = np.ascontiguousarray(np.concatenate(gl, axis=1), dtype=f)
    common["postg"] = np.ascontiguousarray(inputs["ple_post_norm"], dtype=f)
    common["vecs"] = np.ascontiguousarray(np.concatenate([
        np.asarray(inputs["a_q_norm"][0]).ravel(), np.asarray(inputs["a_k_norm"][0]).ravel(),
        np.asarray(inputs["a_sink"][0]).ravel(), np.asarray(inputs["b_q_norm"][0]).ravel(),
        np.asarray(inputs["b_k_norm"][0]).ravel(), np.asarray(inputs["b_lambda"][0]).ravel(),
        np.asarray(inputs["b_subln"][0]).ravel()]), dtype=f)
    common["ident"] = np.eye(128, dtype=f)
    kk = np.arange(128)[:, None]
    jj = np.arange(384)[None, :]
    dA = np.abs(jj - 128 - kk)
    jr = jj // 128
    qq = jj % 128
    dA = np.abs(qq - kk + (1 - jr) * 128)
    dA = np.where(dA <= 128, dA, 60000)
    common["distA"] = dA.astype(np.uint16)
    j2 = np.arange(GW)[None, :]
    common["distB"] = np.abs(j2 - kk - GOFF).astype(np.uint16)
    maps = []
    for b in range(8):
        m = dict(common)
        m["x"] = np.ascontiguousarray(x[b])
        m["pT"] = np.ascontiguousarray(np.stack([p[0, b].T, p[1, b].T]), dtype=f)
        maps.append(m)
    return maps


_CACHE = {}


def kernel(**inputs):
    stop = os.environ.get("KSTOP") or None
    ncores = int(os.environ.get("KCORES", "8"))
    maps = _host_inputs(inputs)[:ncores]
    if stop not in _CACHE:
        _CACHE[stop] = build(stop)
    nc = _CACHE[stop]
    res = run_bass_kernel_spmd(nc, maps, core_ids=list(range(ncores)))
    outs = [r["out"] for r in res.results]
    while len(outs) < 8:
        outs.append(np.zeros_like(outs[0]))
    return np.stack(outs, axis=0).astype(np.float32)
```
